# Optimizing a Trainium2 kernel written in Bass

```python
import math
import jax, jax.numpy as jnp
from jax import lax
import numpy as np

D_MODEL = 1024
BATCH = 16
SEQ = 4096
DEPTH = 4

GRID_W = 64
CTX_LEN = 256
HEAD_DIM = 64
ATTN_WIDTH = D_MODEL // 2
N_HEADS = ATTN_WIDTH // HEAD_DIM
KV_HEADS = N_HEADS // 4
Q_PER_KV = N_HEADS // KV_HEADS
KV_WIDTH = KV_HEADS * HEAD_DIM
WINDOW = 128
ATTN_BLOCK = WINDOW
ATTN_SCALE = HEAD_DIM ** -0.5
ROPE_BASE = 10000.0
ROPE_PAIRS = HEAD_DIM // 4
SSM_WIDTH = D_MODEL // 4
SSM_GROUP = 16
SSM_GROUPS = SSM_WIDTH // SSM_GROUP
SSM_STATE = 64
LOG_DT_MIN = math.log(1e-3)
LOG_DT_MAX = math.log(1e-1)
CONV_WIDTH = D_MODEL // 4
CONV_K = 3
MIX_WIDTH = ATTN_WIDTH + SSM_WIDTH + CONV_WIDTH
Q_END = ATTN_WIDTH
K_END = Q_END + KV_WIDTH
V_END = K_END + KV_WIDTH
U_END = V_END + SSM_WIDTH
GB_END = U_END + CONV_WIDTH
GC_END = GB_END + CONV_WIDTH
IN_COLS = GC_END + CONV_WIDTH
SPLITS = (Q_END, K_END, V_END, U_END, GB_END, GC_END)
D_FF = ((8 * D_MODEL // 3 + 127) // 128) * 128
MACARON = 0.5
N_MOD = 9
EPS = 1e-6
NEG_INF = -1e30

kernel_name = 'hybrid_headgroup_diffusion_trunk'


def _rms_norm(x, g):
    xf = x.astype(jnp.float32)
    xf = xf * lax.rsqrt(jnp.mean(xf * xf, axis=-1, keepdims=True) + EPS)
    return (xf * g.astype(jnp.float32)).astype(x.dtype)


def _modulate(x, g, shift, scale):
    return _rms_norm(x, g) * (1 + scale) + shift


def _gated_post(y, g, gate):
    return gate * _rms_norm(y, g)


def _swiglu(h, wg, wu, wd):
    return (jax.nn.silu(h @ wg) * (h @ wu)) @ wd


def _ffn_sublayer(x, m, s, g_pre, g_post, wg, wu, wd):
    h = _modulate(x, g_pre, m[:, 3 * s], m[:, 3 * s + 1])
    return _gated_post(_swiglu(h, wg, wu, wd), g_post, m[:, 3 * s + 2])


def _axial_rope_tables(length):
    rows = length // GRID_W
    row = jnp.repeat(jnp.arange(rows, dtype=jnp.float32), GRID_W)
    col = jnp.tile(jnp.arange(GRID_W, dtype=jnp.float32), rows)
    inv_freq = ROPE_BASE ** (-jnp.arange(ROPE_PAIRS, dtype=jnp.float32) / ROPE_PAIRS)
    ang = jnp.stack([row[:, None] * inv_freq, col[:, None] * inv_freq], axis=1)
    return jnp.cos(ang), jnp.sin(ang)


def _apply_rope(t, cos, sin):
    tf = t.astype(jnp.float32).reshape(t.shape[:-1] + (2, 2, ROPE_PAIRS))
    t1, t2 = tf[..., 0, :], tf[..., 1, :]
    cs, sn = cos[None, :, None], sin[None, :, None]
    out = jnp.stack([t1 * cs - t2 * sn, t2 * cs + t1 * sn], axis=-2)
    return out.reshape(t.shape).astype(t.dtype)


def _band(t, nb):
    b_, length = t.shape[:2]
    tp = jnp.pad(t, ((0, 0), (ATTN_BLOCK, ATTN_BLOCK), (0, 0), (0, 0)))
    return jnp.concatenate(
        [tp[:, o * ATTN_BLOCK:o * ATTN_BLOCK + length].reshape((b_, nb, ATTN_BLOCK) + t.shape[2:]) for o in range(3)],
        axis=2)


def _window_attention(q, k, v, kc, vc, sink):
    b_, length = q.shape[:2]
    nb = length // ATTN_BLOCK
    n_loc, n_ctx = 3 * ATTN_BLOCK, kc.shape[1]
    qb = (q * ATTN_SCALE).reshape(b_, nb, ATTN_BLOCK, KV_HEADS, Q_PER_KV, HEAD_DIM)
    kb, vb = _band(k, nb), _band(v, nb)
    s_loc = jnp.einsum('bnqkgd,bnjkd->bnkgqj', qb, kb).astype(jnp.float32)
    s_ctx = jnp.einsum('bnqkgd,bckd->bnkgqc', qb, kc).astype(jnp.float32)
    qi = jnp.arange(ATTN_BLOCK)[:, None]
    kj = jnp.arange(n_loc)[None, :]
    kpos = jnp.arange(nb)[:, None, None] * ATTN_BLOCK + kj[None] - ATTN_BLOCK
    valid = (jnp.abs(kj - ATTN_BLOCK - qi) <= WINDOW)[None] & (kpos >= 0) & (kpos < length)
    s_loc = jnp.where(valid[None, :, None, None], s_loc, NEG_INF)
    sink_l = jnp.broadcast_to(sink.astype(jnp.float32).reshape(1, 1, KV_HEADS, Q_PER_KV, 1, 1), s_loc.shape[:-1] + (1,))
    p = jax.nn.softmax(jnp.concatenate([s_loc, s_ctx, sink_l], axis=-1), axis=-1).astype(v.dtype)
    out = (jnp.einsum('bnkgqj,bnjkd->bnqkgd', p[..., :n_loc], vb)
           + jnp.einsum('bnkgqc,bckd->bnqkgd', p[..., n_loc:n_loc + n_ctx], vc))
    return out.reshape(b_, length, ATTN_WIDTH)


def _context_attention(qc, kc, vc, sink):
    b_, n_ctx = qc.shape[:2]
    qg = (qc * ATTN_SCALE).reshape(b_, n_ctx, KV_HEADS, Q_PER_KV, HEAD_DIM)
    s = jnp.einsum('bqkgd,bckd->bkgqc', qg, kc).astype(jnp.float32)
    sink_c = jnp.broadcast_to(sink.astype(jnp.float32).reshape(1, KV_HEADS, Q_PER_KV, 1, 1), s.shape[:-1] + (1,))
    p = jax.nn.softmax(jnp.concatenate([s, sink_c], axis=-1), axis=-1)[..., :n_ctx].astype(vc.dtype)
    return jnp.einsum('bkgqc,bckd->bqkgd', p, vc).reshape(b_, n_ctx, ATTN_WIDTH)


def _complex_combine(left, right):
    a1r, a1i, b1r, b1i = left
    a2r, a2i, b2r, b2i = right
    return (a2r * a1r - a2i * a1i,
            a2r * a1i + a2i * a1r,
            a2r * b1r - a2i * b1i + b2r,
            a2r * b1i + a2i * b1r + b2i)


def _zoh(lam_re, lam_im, log_step, b_re, b_im):
    lr, li = lam_re.astype(jnp.float32), lam_im.astype(jnp.float32)
    dt = jnp.exp(log_step.astype(jnp.float32))[:, None]
    mag = jnp.exp(lr * dt)
    ar, ai = mag * jnp.cos(li * dt), mag * jnp.sin(li * dt)
    den = lr * lr + li * li
    gr = ((ar - 1) * lr + ai * li) / den
    gi = (ai * lr - (ar - 1) * li) / den
    br, bi = b_re.astype(jnp.float32), b_im.astype(jnp.float32)
    bbr = gr[..., None] * br - gi[..., None] * bi
    bbi = gr[..., None] * bi + gi[..., None] * br
    return ar, ai, bbr, bbi


def _diag_scan(u, ar, ai, bbr, bbi, init, reverse):
    length = u.shape[1]
    bu_r = jnp.einsum('gph,blgh->blgp', bbr, u)
    bu_i = jnp.einsum('gph,blgh->blgp', bbi, u)
    if init is not None:
        ir, ii = init
        pos = length - 1 if reverse else 0
        bu_r = bu_r.at[:, pos].add(ar * ir - ai * ii)
        bu_i = bu_i.at[:, pos].add(ar * ii + ai * ir)
    a_r = jnp.broadcast_to(ar, (1, length) + ar.shape)
    a_i = jnp.broadcast_to(ai, (1, length) + ai.shape)
    _, _, s_r, s_i = lax.associative_scan(_complex_combine, (a_r, a_i, bu_r, bu_i), reverse=reverse, axis=1)
    return s_r, s_i


def _readout(cr, ci, sr, si):
    return jnp.einsum('ghp,blgp->blgh', cr, sr) - jnp.einsum('ghp,blgp->blgh', ci, si)


def _glu(y, w, b):
    g = jax.nn.gelu(y)
    return g * jax.nn.sigmoid(g @ w + b)


def _s5_mixer(u, uc, lam_re, lam_im, log_step, b_re, b_im, c_re, c_im, d_skip, w_glu, b_glu, ctx_out):
    b_, length, _ = u.shape
    n_ctx = uc.shape[1]
    uf = u.astype(jnp.float32).reshape(b_, length, SSM_GROUPS, SSM_GROUP)
    ucf = uc.astype(jnp.float32).reshape(b_, n_ctx, SSM_GROUPS, SSM_GROUP)
    d = d_skip.astype(jnp.float32).reshape(SSM_GROUPS, SSM_GROUP)
    y = d * uf
    yc = d * ucf if ctx_out else None
    for dirn in range(2):
        rev = dirn == 1
        ar, ai, bbr, bbi = _zoh(lam_re[dirn], lam_im[dirn], log_step[dirn], b_re[dirn], b_im[dirn])
        cr, ci = c_re[dirn].astype(jnp.float32), c_im[dirn].astype(jnp.float32)
        sc_r, sc_i = _diag_scan(ucf, ar, ai, bbr, bbi, None, rev)
        end = 0 if rev else -1
        s_r, s_i = _diag_scan(uf, ar, ai, bbr, bbi, (sc_r[:, end], sc_i[:, end]), rev)
        y = y + _readout(cr, ci, s_r, s_i)
        if ctx_out:
            yc = yc + _readout(cr, ci, sc_r, sc_i)
    out = _glu(y.reshape(b_, length, SSM_WIDTH).astype(u.dtype), w_glu, b_glu)
    out_c = _glu(yc.reshape(b_, n_ctx, SSM_WIDTH).astype(uc.dtype), w_glu, b_glu) if ctx_out else None
    return out, out_c


def _short_conv(t, w):
    tp = jnp.pad(t, ((0, 0), (1, 1), (0, 0)))
    return tp[:, :-2] * w[0] + tp[:, 1:-1] * w[1] + tp[:, 2:] * w[2]


def _token_mixer(h, hc, cos, sin, w_in, w_out, sink, lam_re, lam_im, log_step, b_re, b_im, c_re, c_im,
                 d_skip, w_glu, b_glu, conv_w, ctx_out):
    b_, length, _ = h.shape
    n_ctx = hc.shape[1]
    q, k, v, u, gb, gc, z = jnp.split(h @ w_in, SPLITS, axis=-1)
    q = _apply_rope(q.reshape(b_, length, N_HEADS, HEAD_DIM), cos, sin)
    k = _apply_rope(k.reshape(b_, length, KV_HEADS, HEAD_DIM), cos, sin)
    v = v.reshape(b_, length, KV_HEADS, HEAD_DIM)
    kc, vc, uc = jnp.split(hc @ w_in[:, Q_END:U_END], (KV_WIDTH, 2 * KV_WIDTH), axis=-1)
    kc = kc.reshape(b_, n_ctx, KV_HEADS, HEAD_DIM)
    vc = vc.reshape(b_, n_ctx, KV_HEADS, HEAD_DIM)
    attn = _window_attention(q, k, v, kc, vc, sink)
    ssm, ssm_c = _s5_mixer(u, uc, lam_re, lam_im, log_step, b_re, b_im, c_re, c_im, d_skip, w_glu, b_glu, ctx_out)
    conv = gb * _short_conv(gc * z, conv_w)
    y = jnp.concatenate([attn, ssm, conv], axis=-1) @ w_out
    if not ctx_out:
        return y, None
    qc = (hc @ w_in[:, :Q_END]).reshape(b_, n_ctx, N_HEADS, HEAD_DIM)
    gbc, gcc, zc = jnp.split(hc @ w_in[:, U_END:], (CONV_WIDTH, 2 * CONV_WIDTH), axis=-1)
    yc = jnp.concatenate([_context_attention(qc, kc, vc, sink), ssm_c, gbc * _short_conv(gcc * zc, conv_w)], axis=-1) @ w_out
    return y, yc


def setup_inputs(seed: int = 0) -> dict:
    key = jax.random.key(seed)
    ks = jax.random.split(key, 26)
    f32 = jnp.float32

    def nrm(k, shape, s):
        return jax.random.normal(k, shape, f32) * s

    sg = (DEPTH, 2, SSM_GROUPS, SSM_STATE)
    n_idx = jnp.arange(SSM_STATE, dtype=f32)
    return {
        'x': nrm(ks[0], (BATCH, SEQ, D_MODEL), 1.0),
        'c': nrm(ks[1], (BATCH, D_MODEL), 1.0),
        'ctx': nrm(ks[2], (BATCH, CTX_LEN, D_MODEL), 1.0),
        'c_ctx': nrm(ks[3], (D_MODEL,), 1.0),
        'w_ada': nrm(ks[4], (DEPTH, D_MODEL, N_MOD * D_MODEL), 0.5 * D_MODEL ** -0.5),
        'b_ada': nrm(ks[5], (DEPTH, N_MOD * D_MODEL), 0.02),
        'norm_pre': 1.0 + nrm(ks[6], (DEPTH, 3, D_MODEL), 0.02),
        'norm_post': 1.0 + nrm(ks[7], (DEPTH, 3, D_MODEL), 0.02),
        'ffn_w_gate': nrm(ks[8], (DEPTH, 2, D_MODEL, D_FF), D_MODEL ** -0.5),
        'ffn_w_up': nrm(ks[9], (DEPTH, 2, D_MODEL, D_FF), D_MODEL ** -0.5),
        'ffn_w_down': nrm(ks[10], (DEPTH, 2, D_FF, D_MODEL), D_FF ** -0.5),
        'w_in': nrm(ks[11], (DEPTH, D_MODEL, IN_COLS), D_MODEL ** -0.5),
        'w_out': nrm(ks[12], (DEPTH, MIX_WIDTH, D_MODEL), MIX_WIDTH ** -0.5),
        'attn_sink': nrm(ks[13], (DEPTH, N_HEADS), 1.0),
        'ssm_lambda_re': -0.5 + nrm(ks[14], sg, 0.01),
        'ssm_lambda_im': jnp.pi * n_idx + nrm(ks[15], sg, 0.01),
        'ssm_log_step': jax.random.uniform(ks[16], (DEPTH, 2, SSM_GROUPS), f32, LOG_DT_MIN, LOG_DT_MAX),
        'ssm_b_re': nrm(ks[17], sg + (SSM_GROUP,), (2 * SSM_GROUP) ** -0.5),
        'ssm_b_im': nrm(ks[18], sg + (SSM_GROUP,), (2 * SSM_GROUP) ** -0.5),
        'ssm_c_re': nrm(ks[19], (DEPTH, 2, SSM_GROUPS, SSM_GROUP, SSM_STATE), SSM_STATE ** -0.5),
        'ssm_c_im': nrm(ks[20], (DEPTH, 2, SSM_GROUPS, SSM_GROUP, SSM_STATE), SSM_STATE ** -0.5),
        'ssm_d': nrm(ks[21], (DEPTH, SSM_WIDTH), 1.0),
        'ssm_w_glu': nrm(ks[22], (DEPTH, SSM_WIDTH, SSM_WIDTH), SSM_WIDTH ** -0.5),
        'ssm_b_glu': nrm(ks[23], (DEPTH, SSM_WIDTH), 0.02),
        'conv_w': nrm(ks[24], (DEPTH, CONV_K, CONV_WIDTH), CONV_K ** -0.5),
    }


def reference(x, c, ctx, c_ctx, w_ada, b_ada, norm_pre, norm_post, ffn_w_gate, ffn_w_up, ffn_w_down,
              w_in, w_out, attn_sink, ssm_lambda_re, ssm_lambda_im, ssm_log_step, ssm_b_re, ssm_b_im,
              ssm_c_re, ssm_c_im, ssm_d, ssm_w_glu, ssm_b_glu, conv_w):
    b_, length, _ = x.shape
    cos, sin = _axial_rope_tables(length)
    silu_c = jax.nn.silu(c)
    silu_cc = jax.nn.silu(c_ctx)
    xc = ctx
    for l in range(DEPTH):
        last = l == DEPTH - 1
        m = (silu_c @ w_ada[l] + b_ada[l]).reshape(b_, N_MOD, 1, D_MODEL)
        mc = (silu_cc @ w_ada[l] + b_ada[l]).reshape(1, N_MOD, 1, D_MODEL)
        f0 = (norm_pre[l, 0], norm_post[l, 0], ffn_w_gate[l, 0], ffn_w_up[l, 0], ffn_w_down[l, 0])
        x = x + MACARON * _ffn_sublayer(x, m, 0, *f0)
        xc = xc + MACARON * _ffn_sublayer(xc, mc, 0, *f0)
        h = _modulate(x, norm_pre[l, 1], m[:, 3], m[:, 4])
        hc = _modulate(xc, norm_pre[l, 1], mc[:, 3], mc[:, 4])
        y, yc = _token_mixer(h, hc, cos, sin, w_in[l], w_out[l], attn_sink[l],
                             ssm_lambda_re[l], ssm_lambda_im[l], ssm_log_step[l], ssm_b_re[l], ssm_b_im[l],
                             ssm_c_re[l], ssm_c_im[l], ssm_d[l], ssm_w_glu[l], ssm_b_glu[l], conv_w[l],
                             not last)
        x = x + _gated_post(y, norm_post[l, 1], m[:, 5])
        f1 = (norm_pre[l, 2], norm_post[l, 2], ffn_w_gate[l, 1], ffn_w_up[l, 1], ffn_w_down[l, 1])
        x = x + MACARON * _ffn_sublayer(x, m, 2, *f1)
        if not last:
            xc = xc + _gated_post(yc, norm_post[l, 1], mc[:, 5])
            xc = xc + MACARON * _ffn_sublayer(xc, mc, 2, *f1)
    return x
```

```python
import math
import contextlib
import numpy as np
import ml_dtypes
import concourse.bass as bass
import concourse.mybir as mybir
from concourse.bass_utils import run_bass_kernel_spmd

F32 = mybir.dt.float32
BF16 = mybir.dt.bfloat16
I32 = mybir.dt.int32
AF = mybir.ActivationFunctionType
ALU = mybir.AluOpType

D = 1024
KC = 8
FF = 2816
FC = 22
NCTX = 256
DEPTH = 4
SEQ = 4096
NB = 2
EPS = 1e-6
TWO_PI = 2.0 * math.pi

EOBJ = {"pe": "tensor", "act": "scalar", "dve": "vector", "pool": "gpsimd", "sp": "sync"}


class KB:
    def __init__(self, nc):
        self.nc = nc
        self.sem = {}
        self.cnt = {}
        self.waited = {e: {} for e in EOBJ}
        self.res = {}
        for e in EOBJ:
            self._mk("eng_" + e)

    def _mk(self, s):
        self.sem[s] = self.nc.alloc_semaphore(s)
        self.cnt[s] = 0

    def eng(self, e):
        return getattr(self.nc, EOBJ[e])

    def _r(self, k):
        r = self.res.get(k)
        if r is None:
            r = self.res[k] = [{}, {}]
        return r

    PSUMK = {"pm", "pmt", "ptr", "pg", "pd", "pq", "pqt", "pz", "pu", "pt", "pw", "pz0", "pz1", "py", "pgt", "pgl",
             "pst", "pnum", "pden"}

    def _deps(self, reads, writes):
        d = {}
        for k in reads:
            r = self._r(k)
            for s, v in r[0].items():
                if d.get(s, 0) < v:
                    d[s] = v
            if (k if isinstance(k, str) else k[0]) in self.PSUMK:
                for s, v in r[1].items():
                    if d.get(s, 0) < v:
                        d[s] = v
        for k in writes:
            r = self._r(k)
            for m in r:
                for s, v in m.items():
                    if d.get(s, 0) < v:
                        d[s] = v
        return d

    def _waits(self, e, deps, skip=None):
        w = self.waited[e]
        for s, v in deps.items():
            if s == skip:
                continue
            if w.get(s, 0) < v:
                w[s] = v
                self.eng(e).wait_ge(self.sem[s], v)

    def _rec(self, reads, writes, s, v):
        for k in reads:
            r = self._r(k)[1]
            if r.get(s, 0) < v:
                r[s] = v
        for k in writes:
            r = self._r(k)
            r[0] = {s: v}
            r[1] = {}

    def op(self, e, fn, reads=(), writes=()):
        deps = self._deps(reads, writes)
        self._waits(e, deps, skip=("eng_pe" if e == "pe" else None))
        ins = fn(self.eng(e))
        s = "eng_" + e
        self.cnt[s] += 1
        ins.then_inc(self.sem[s], 1)
        self._rec(reads, writes, s, self.cnt[s])

    def dma(self, q, out, in_, reads=(), writes=(), anchor=None):
        s = "dma_" + anchor
        if s not in self.sem:
            self._mk(s)
        deps = self._deps(reads, writes)
        self._waits(q, deps, skip=s)
        ins = self.eng(q).dma_start(out=out, in_=in_)
        self.cnt[s] += 16
        ins.then_inc(self.sem[s], 16)
        self._rec(reads, writes, s, self.cnt[s])

    def barrier(self):
        for e in EOBJ:
            w = self.waited[e]
            for s, v in self.cnt.items():
                if v > 0 and w.get(s, 0) < v:
                    w[s] = v
                    self.eng(e).wait_ge(self.sem[s], v)
        self.res = {}


def build(L=SEQ, depth=DEPTH, dbg=False):
    nc = bass.Bass("TRN2", target_bir_lowering=False)
    kb = KB(nc)
    NP = NCTX + L
    NT = L // 128
    NCH = NP // 8
    NCC = NCTX // 8
    NLC = L // 8
    NKC = NP // 128
    nlev = int(math.ceil(math.log2(NCH)))

    def din(name, shape, dt=F32):
        return nc.dram_tensor(name, list(shape), dt, kind="ExternalInput").ap()

    def dscr(name, shape, dt):
        return nc.dram_tensor(name, list(shape), dt, kind="Internal").ap()

    x_in = din("x", [NB, L, D])
    ctx_in = din("ctx", [NB, NCTX, D])
    cT_in = din("cT", [128, KC, 3])
    w_ada = din("w_ada", [depth, D, 9 * D])
    b_ada = din("b_ada", [depth, 9 * D])
    gpreT_in = din("gpreT", [128, depth * 3, KC])
    norm_post = din("norm_post", [depth, 3, D])
    wgate = din("ffn_w_gate", [depth, 2, D, FF])
    wup = din("ffn_w_up", [depth, 2, D, FF])
    wdown = din("ffn_w_down", [depth, 2, FF, D])
    w_in = din("w_in", [depth, D, 1792])
    w_out = din("w_out", [depth, D, D])
    sink_in = din("sink", [depth, 128, 8])
    lam_re = din("lam_re", [depth, 128, 16])
    lam_im = din("lam_im", [depth, 128, 16])
    lstep = din("lstep", [depth, 128, 16])
    bre_in = din("b_re", [depth, 128, 256])
    bim_in = din("b_im", [depth, 128, 256])
    cre_in = din("c_re", [depth, 128, 256])
    cim_in = din("c_im", [depth, 128, 256])
    dsk_in = din("d_skip", [depth, 128, 16])
    wglu_in = din("w_glu", [depth, 256, 256])
    bglu_in = din("b_glu", [depth, 128, 2])
    convw_in = din("conv_w", [depth, 128, 6])
    ident_in = din("ident", [128, 128])
    mprev_in = din("m_prev", [128, 128])
    mnext_in = din("m_next", [128, 128])
    tmf_in = din("tmask_f", [128, 128])
    tmb_in = din("tmask_b", [128, 128])
    ghm_in = din("gh_mask", [128, 2])
    rope_in = din("rope", [128, NT, 64])
    y_out = nc.dram_tensor("y", [NB, L, D], F32, kind="ExternalOutput").ap()

    xres = dscr("xres", [NB, NP, D], F32)
    modr = dscr("modr", [depth, 3, 9, D], F32)
    qT_d = dscr("qT_d", [NB, 4, 128, NP], BF16)
    kT_d = dscr("kT_d", [NB, 128, NP], BF16)
    v_d = dscr("v_d", [NB, NP, 128], BF16)
    U_d = dscr("U_d", [NB, 128, 16, NCH], BF16)
    gb_d = dscr("gb_d", [NB, 2, 128, NP], BF16)
    gz_d = dscr("gz_d", [NB, 2, 128, NP], BF16)
    mix_d = dscr("mix_d", [NB, 8, 128, NP], BF16)

    ES = contextlib.ExitStack

    uid = [0]

    def S(es, name, shape, dt=F32):
        uid[0] += 1
        return es.enter_context(nc.sbuf_tensor("sb%d_%s" % (uid[0], name), list(shape), dt))

    def PS(es, name, shape, dt=F32):
        uid[0] += 1
        return es.enter_context(nc.psum_tensor("ps%d_%s" % (uid[0], name), list(shape), dt))

    glob = ES()
    with glob:
        ident_f = S(glob, "ident_f", [128, 128])
        ident_b = S(glob, "ident_b", [128, 128], BF16)
        modT = S(glob, "modT", [128, depth * 9, KC, 3])
        gpreT = S(glob, "gpreT", [128, depth * 3, KC])
        kb.dma("sp", ident_f[:], ident_in[:, :], writes=["ident_f"], anchor="c0")
        kb.dma("sp", gpreT[:], gpreT_in[:, :, :], writes=["gpreT"], anchor="c1")
        kb.op("dve", lambda e: e.tensor_copy(ident_b[:], ident_f[:]), reads=["ident_f"], writes=["ident_b"])

        for b in range(NB):
            kb.dma("sp", xres[b, 0:NCTX, :], ctx_in[b, :, :], writes=[("xres", b, "c")], anchor="cp")
            kb.dma("sp", xres[b, NCTX:NP, :], x_in[b, :, :], writes=[("xres", b, "x")], anchor="cp")

        with ES() as es:
            cT = S(es, "cT", [128, KC, 3])
            scT = S(es, "scT", [128, KC, 3])
            wa = [S(es, "wa%d" % i, [128, KC, D]) for i in range(2)]
            bias3 = [S(es, "bias3_%d" % i, [3, D]) for i in range(2)]
            mrow = [S(es, "mrow%d" % i, [3, D]) for i in range(2)]
            pm = [PS(es, "pm%d" % i, [128, 1024]) for i in range(2)]
            pmt = [PS(es, "pmt%d" % i, [128, 512]) for i in range(2)]
            kb.dma("sp", cT[:], cT_in[:, :, :], writes=["cT"], anchor="c2")
            kb.op("act", lambda e: e.activation(out=scT[:], in_=cT[:], func=AF.Silu), reads=["cT"], writes=["scT"])
            it = 0
            for l in range(depth):
                for j in range(9):
                    sl = it % 2
                    it += 1
                    kb.dma("sp" if sl == 0 else "pool", wa[sl][:], w_ada[l, :, j * D:(j + 1) * D].rearrange("(k p) c -> p k c", p=128),
                           writes=[("wa", sl)], anchor="%swa%d" % ("" if sl == 0 else "p", sl))
                    kb.dma("sp", bias3[sl][:], b_ada[l:l + 1, j * D:(j + 1) * D].partition_broadcast(3),
                           writes=[("bias3", sl)], anchor="b3%d" % sl)
                    for nh in range(2):
                        for k in range(KC):
                            kb.op("pe", lambda e, k=k, nh=nh, sl=sl: e.matmul(
                                pm[sl][0:3, nh * 512:(nh + 1) * 512], lhsT=scT[:, k, :],
                                rhs=wa[sl][:, k, nh * 512:(nh + 1) * 512], start=(k == 0), stop=(k == KC - 1)),
                                reads=["scT", ("wa", sl)], writes=[("pm", sl)])
                    kb.op("dve", lambda e, sl=sl: e.tensor_tensor(mrow[sl][:], pm[sl][0:3, :], bias3[sl][:], ALU.add),
                          reads=[("pm", sl), ("bias3", sl)], writes=[("mrow", sl)])
                    kb.dma("sp", modr[l, :, j, :], mrow[sl][:], reads=[("mrow", sl)], writes=[("modr", l, j)],
                           anchor="mst%d" % sl)
                    for k in range(KC):
                        kb.op("pe", lambda e, k=k, sl=sl: e.transpose(
                            pmt[sl][:, k * 3:k * 3 + 3], mrow[sl][0:3, k * 128:(k + 1) * 128], ident_f[0:3, 0:3]),
                            reads=[("mrow", sl), "ident_f"], writes=[("pmt", sl)])
                    kb.op("dve", lambda e, sl=sl, l=l, j=j: e.tensor_copy(
                        modT[:, l * 9 + j, :, :], pmt[sl][:, 0:24].rearrange("p (k r) -> p k r", r=3)),
                        reads=[("pmt", sl)], writes=["modT"])
        kb.barrier()

        def segs(with_ctx=True, with_lat=True):
            out = []
            for b in range(NB):
                if with_ctx:
                    out.append((b, 2, 0, NCTX))
                if with_lat:
                    out.append((b, b, NCTX, L))
            return out

        def load_rowconsts(es, l, s, fac):
            A = S(es, "Acol", [128, 3, KC])
            gp = S(es, "gp_bc", [128, D])
            gt = S(es, "gate_bc", [128, D])
            gv = [S(es, "gvec%d" % r, [128, D]) for r in range(3)]
            for r in range(3):
                kb.op("dve", lambda e, r=r: e.scalar_tensor_tensor(
                    A[:, r, :], modT[:, l * 9 + 3 * s + 1, :, r], 1.0, gpreT[:, l * 3 + s, :], ALU.add, ALU.mult),
                    reads=["modT", "gpreT"], writes=["Acol"])
            kb.dma("sp", gp[:], norm_post[l, s:s + 1, :].partition_broadcast(128), writes=["gp_bc"], anchor="gp")
            for r in range(3):
                kb.dma("sp", gt[:], modr[l, r, 3 * s + 2:3 * s + 3, :].partition_broadcast(128),
                       reads=[("modr", l, 3 * s + 2)], writes=["gate_bc"], anchor="gt")
                kb.op("dve", lambda e, r=r: e.scalar_tensor_tensor(gv[r][:], gt[:], fac, gp[:], ALU.mult, ALU.mult),
                      reads=["gate_bc", "gp_bc"], writes=[("gvec", r)])
            return A, gv

        def norm_to_hT(xin_t, xkey, ntile, r, l, s, A, xn, junk, ss, rstd, ptr, hT, hkey):
            for t in range(ntile):
                kb.op("act", lambda e, t=t: e.activation(out=junk[:], in_=xin_t[:, t, :], func=AF.Square,
                                                        accum_out=ss[:, t:t + 1]),
                      reads=[xkey], writes=["junk", ("ss", t)])
            kb.op("act", lambda e: e.activation(out=rstd[:, 0:ntile], in_=ss[:, 0:ntile], func=AF.Sqrt,
                                                scale=1.0 / D, bias=EPS),
                  reads=[("ss", t) for t in range(ntile)], writes=["rstd"])
            kb.op("dve", lambda e: e.reciprocal(rstd[:, 0:ntile], rstd[:, 0:ntile]), reads=["rstd"], writes=["rstd"])
            for t in range(ntile):
                xs = t % 2
                kb.op("dve", lambda e, t=t, xs=xs: e.tensor_scalar(xn[xs][:], xin_t[:, t, :], rstd[:, t:t + 1], None,
                                                                  ALU.mult),
                      reads=[xkey, "rstd"], writes=[("xn", xs)])
                for k in range(KC):
                    kb.op("pe", lambda e, k=k, xs=xs: e.transpose(ptr[xs][:, k * 128:(k + 1) * 128],
                                                                  xn[xs][:, k * 128:(k + 1) * 128], ident_b[:]),
                          reads=[("xn", xs), "ident_b"], writes=[("ptr", xs)])
                for k in range(KC):
                    kb.op("dve", lambda e, k=k, t=t, xs=xs: e.tensor_scalar(
                        hT[:, k, t * 128:(t + 1) * 128], ptr[xs][:, k * 128:(k + 1) * 128],
                        A[:, r, k:k + 1], modT[:, l * 9 + 3 * s, k, r:r + 1], ALU.mult, ALU.add),
                        reads=[("ptr", xs), "Acol", "modT"], writes=[(hkey, k)])

        def norm_stats(xin_t, xkey, ntile, junk, ss, rstd, tag):
            for t in range(ntile):
                kb.op("act", lambda e, t=t: e.activation(out=junk[:], in_=xin_t[:, t, :], func=AF.Square,
                                                        accum_out=ss[:, t:t + 1]),
                      reads=[xkey], writes=["junk", ("ss", tag, t)])
            kb.op("act", lambda e: e.activation(out=rstd[:, 0:ntile], in_=ss[:, 0:ntile], func=AF.Sqrt,
                                                scale=1.0 / D, bias=EPS),
                  reads=[("ss", tag, t) for t in range(ntile)], writes=[("rstd", tag)])
            kb.op("dve", lambda e: e.reciprocal(rstd[:, 0:ntile], rstd[:, 0:ntile]), reads=[("rstd", tag)],
                  writes=[("rstd", tag)])

        def norm_xn(xin_t, xkey, ntile, xn, rstd, tag):
            for t in range(ntile):
                kb.op("dve", lambda e, t=t: e.tensor_scalar(xn[t][:], xin_t[:, t, :], rstd[:, t:t + 1], None, ALU.mult),
                      reads=[xkey, ("rstd", tag)], writes=[("xn", t)])

        def norm_tr(ntile, xn, ptr):
            for t in range(ntile):
                for k in range(KC):
                    kb.op("pe", lambda e, k=k, t=t: e.transpose(ptr[t][:, k * 128:(k + 1) * 128],
                                                                xn[t][:, k * 128:(k + 1) * 128], ident_b[:]),
                          reads=[("xn", t), "ident_b"], writes=[("ptr", t)])

        def norm_evac(ntile, r, l, s, A, ptr, hT, hkey):
            for t in range(ntile):
                for k in range(KC):
                    kb.op("dve", lambda e, k=k, t=t: e.tensor_scalar(
                        hT[:, k, t * 128:(t + 1) * 128], ptr[t][:, k * 128:(k + 1) * 128],
                        A[:, r, k:k + 1], modT[:, l * 9 + 3 * s, k, r:r + 1], ALU.mult, ALU.add),
                        reads=[("ptr", t), "Acol", "modT"], writes=[(hkey, k)])

        def post_residual(py, pykey, xin_ap, xkey, gv_r, gkey, junk, ss2, rstd2, tt, t):
            import os
            kp = int(os.environ.get("KPOST", "99"))
            if kp <= 0:
                return
            kb.op("act", lambda e: e.activation(out=junk[:], in_=py[:], func=AF.Square, accum_out=ss2[:, t:t + 1]),
                  reads=[pykey], writes=["junk", ("ss2", t)])
            if kp <= 1:
                return
            for hh in range(2):
                kv = os.environ.get("KVAR", "")
                if kv == "a":
                    kb.op("dve", lambda e, hh=hh: e.tensor_copy(tt[:, hh * 512:(hh + 1) * 512], py[:, hh * 512:(hh + 1) * 512]),
                          reads=[pykey, gkey], writes=["tt"])
                elif kv == "b":
                    kb.op("dve", lambda e, hh=hh: e.tensor_tensor(tt[:, hh * 512:(hh + 1) * 512], gv_r[:, hh * 512:(hh + 1) * 512],
                                                                  gv_r[:, hh * 512:(hh + 1) * 512], ALU.mult),
                          reads=[pykey, gkey], writes=["tt"])
                elif kv == "c":
                    kb.op("dve", lambda e, hh=hh: e.tensor_copy(tt[:, hh * 512:(hh + 1) * 512], gv_r[:, hh * 512:(hh + 1) * 512]),
                          reads=[gkey], writes=["tt"])
                else:
                    kb.op("dve", lambda e, hh=hh: e.tensor_tensor(tt[:, hh * 512:(hh + 1) * 512], gv_r[:, hh * 512:(hh + 1) * 512],
                                                                  py[:, hh * 512:(hh + 1) * 512], ALU.mult),
                          reads=[pykey, gkey, ("ss2", t)], writes=["tt"])
            if kp <= 2:
                return
            kb.op("act", lambda e: e.activation(out=rstd2[:, t:t + 1], in_=ss2[:, t:t + 1], func=AF.Sqrt,
                                                scale=1.0 / D, bias=EPS),
                  reads=[("ss2", t)], writes=[("rstd2", t)])
            kb.op("dve", lambda e: e.reciprocal(rstd2[:, t:t + 1], rstd2[:, t:t + 1]),
                  reads=[("rstd2", t)], writes=[("rstd2", t)])
            kb.op("dve", lambda e: e.scalar_tensor_tensor(xin_ap, tt[:], rstd2[:, t:t + 1], xin_ap, ALU.mult, ALU.add),
                  reads=["tt", ("rstd2", t), xkey], writes=[xkey])

        def ffn_phase(l, s, fi, with_ctx, final):
            G = 256
            with ES() as es:
                wg = S(es, "wg", [128, KC, FF], BF16)
                wu = S(es, "wu", [128, KC, FF], BF16)
                wd = S(es, "wd", [128, FC, D], BF16)
                for k in range(KC):
                    kb.dma("pool", wg[:, k, :], wgate[l, fi, k * 128:(k + 1) * 128, :], writes=["wg"], anchor="wg")
                    kb.dma("pool", wu[:, k, :], wup[l, fi, k * 128:(k + 1) * 128, :], writes=["wu"], anchor="wu")
                for f in range(0, FC, 2):
                    kb.dma("pool", wd[:, f:f + 2, :],
                           wdown[l, fi, f * 128:(f + 2) * 128, :].rearrange("(f p) d -> p f d", p=128),
                           writes=["wd"], anchor="wd")
                import os
                sub = int(os.environ.get("KSUB", "99"))
                if sub <= 1:
                    kb.barrier(); return
                A, gv = load_rowconsts(es, l, s, 0.5)
                if sub <= 2:
                    kb.barrier(); return
                xin = [S(es, "xin%d" % i, [128, 2, D]) for i in range(2)]
                xn = [S(es, "xn%d" % i, [128, D], BF16) for i in range(2)]
                junk = S(es, "junk", [128, D], BF16)
                ssl = [S(es, "ss%d" % i, [128, 8]) for i in range(2)]
                rstdl = [S(es, "rstd%d" % i, [128, 8]) for i in range(2)]
                ss2 = S(es, "ss2", [128, 8]); rstd2 = S(es, "rstd2", [128, 8])
                hTl = [S(es, "hT%d" % i, [128, KC, G], BF16) for i in range(2)]
                actT = S(es, "actT", [128, FC, G], BF16)
                sg = [S(es, "sg%d" % i, [128, G]) for i in range(2)]
                tt = S(es, "tt", [128, D])
                ptr = [PS(es, "ptr%d" % i, [128, 1024], BF16) for i in range(2)]
                pg = [PS(es, "pg%d" % i, [128, 512]) for i in range(2)]
                pd = [PS(es, "pd%d" % i, [128, 1024]) for i in range(2)]
                groups = []
                for (b, r, p0, n) in segs(with_ctx, True):
                    for g0 in range(0, n, G):
                        groups.append((b, r, p0 + g0))
                groups = groups[:int(os.environ.get("KGRP", "9999"))]

                def load(gi):
                    b, r, pos = groups[gi]
                    sl = gi % 2
                    kb.dma("sp", xin[sl][:], xres[b, pos:pos + G, :].rearrange("(t p) d -> p t d", p=128),
                           writes=[("xin", sl)], anchor="xl%d" % sl)

                def nstats(gi):
                    sl = gi % 2
                    norm_stats(xin[sl], ("xin", sl), 2, junk, ssl[sl], rstdl[sl], sl)

                def nxn(gi):
                    sl = gi % 2
                    norm_xn(xin[sl], ("xin", sl), 2, xn, rstdl[sl], sl)

                def nevac(gi):
                    sl = gi % 2
                    norm_evac(2, groups[gi][1], l, s, A, ptr, hTl[sl], ("hT", sl))

                load(0)
                if len(groups) > 1:
                    load(1)
                nstats(0); nxn(0); norm_tr(2, xn, ptr); nevac(0)
                for gi, (b, r, pos) in enumerate(groups):
                    sl = gi % 2
                    hT = hTl[sl]
                    nxt = gi + 1 < len(groups)
                    for f in range(FC):
                        ps = f % 2
                        for half, w in ((0, wg), (1, wu)):
                            for k in range(KC):
                                kb.op("pe", lambda e, k=k, f=f, ps=ps, half=half, w=w: e.matmul(
                                    pg[ps][:, half * G:(half + 1) * G], lhsT=w[:, k, f * 128:(f + 1) * 128],
                                    rhs=hT[:, k, :], start=(k == 0), stop=(k == KC - 1)),
                                    reads=[(("hT", sl), k), "wg" if half == 0 else "wu"], writes=[("pg", ps)])
                        kb.op("act", lambda e, ps=ps: e.activation(out=sg[ps][:], in_=pg[ps][:, 0:G], func=AF.Silu),
                              reads=[("pg", ps)], writes=[("sg", ps)])
                        kb.op("dve", lambda e, ps=ps, f=f: e.tensor_tensor(actT[:, f, :], sg[ps][:], pg[ps][:, G:2 * G],
                                                                        ALU.mult),
                              reads=[("sg", ps), ("pg", ps)], writes=[("actT", f)])
                        if nxt and f == 6:
                            nstats(gi + 1)
                        if nxt and f == 12:
                            nxn(gi + 1)
                    if nxt:
                        norm_tr(2, xn, ptr)
                    for t in range(2):
                        for nh in range(2):
                            for f in range(FC):
                                kb.op("pe", lambda e, t=t, nh=nh, f=f: e.matmul(
                                    pd[t][:, nh * 512:(nh + 1) * 512], lhsT=actT[:, f, t * 128:(t + 1) * 128],
                                    rhs=wd[:, f, nh * 512:(nh + 1) * 512], start=(f == 0), stop=(f == FC - 1)),
                                    reads=[("actT", f), "wd"], writes=[("pd", t)])
                        if nxt and t == 0:
                            nevac(gi + 1)
                        post_residual(pd[t], ("pd", t), xin[sl][:, t, :], ("xin", sl), gv[r], ("gvec", r), junk, ss2, rstd2, tt, t)
                    if final and pos >= NCTX:
                        dst = y_out[b, pos - NCTX:pos - NCTX + G, :]
                    else:
                        dst = xres[b, pos:pos + G, :]
                    kb.dma("sp", dst.rearrange("(t p) d -> p t d", p=128), xin[sl][:],
                           reads=[("xin", sl)], writes=[("xst", b, pos)], anchor="xs%d" % sl)
                    if gi + 2 < len(groups):
                        load(gi + 2)
            kb.barrier()

        def mix_proj(l, ctx_out):
            with ES() as es:
                wi = S(es, "wi", [128, KC, 1792], BF16)
                for k in range(KC):
                    kb.dma("pool", wi[:, k, :], w_in[l, k * 128:(k + 1) * 128, :], writes=["wi"], anchor="wi")
                rope = S(es, "rope", [128, NT, 64])
                kb.dma("sp", rope[:], rope_in[:, :, :], writes=["rope"], anchor="rope")
                A = S(es, "Acol", [128, 3, KC])
                for r in range(3):
                    kb.op("dve", lambda e, r=r: e.scalar_tensor_tensor(
                        A[:, r, :], modT[:, l * 9 + 4, :, r], 1.0, gpreT[:, l * 3 + 1, :], ALU.add, ALU.mult),
                        reads=["modT", "gpreT"], writes=["Acol"])
                xinl = [S(es, "xin%d" % i, [128, 8, D]) for i in range(2)]
                xn = [S(es, "xn%d" % i, [128, D], BF16) for i in range(2)]
                junk = S(es, "junk", [128, D], BF16)
                ss = S(es, "ss", [128, 8]); rstd = S(es, "rstd", [128, 8])
                hTl = [S(es, "hT%d" % i, [128, KC, 1024], BF16) for i in range(2)]
                qk_tm = S(es, "qk_tm", [128, 640], BF16)
                v_tm = S(es, "v_tm", [128, 128], BF16)
                ra = S(es, "ra", [128, 320]); rb = S(es, "rb", [128, 320])
                rc = S(es, "rc", [128, 320]); rd = S(es, "rd", [128, 320])
                qkT = S(es, "qkT", [128, 5, 128], BF16)
                gb_sb = S(es, "gb_sb", [128, 2, 512], BF16)
                gc_sb = S(es, "gc_sb", [128, 2, 512], BF16)
                gz_sb = S(es, "gz_sb", [128, 2, 512], BF16)
                UT = S(es, "UT", [128, 16, 8, 16], BF16)
                Usb = S(es, "Usb", [128, 16, 128], BF16)
                ptr = [PS(es, "ptr%d" % i, [128, 1024], BF16) for i in range(2)]
                pq = PS(es, "pq", [128, 1024])
                pqt = PS(es, "pqt", [128, 1024], BF16)
                pz = PS(es, "pz", [128, 512])
                pu = PS(es, "pu", [128, 1024])
                pUT = pu[:, :].bitcast(BF16)
                stl = []
                for (b, r, p0, n) in segs(True, True):
                    for st0 in range(0, n, 1024):
                        stl.append((b, r, p0 + st0, min(8, (n - st0) // 128)))

                def st_load(si):
                    b_, r_, pos_, nt_ = stl[si]
                    kb.dma("pool", xinl[si % 2][:, 0:nt_, :],
                           xres[b_, pos_:pos_ + nt_ * 128, :].rearrange("(t p) d -> p t d", p=128),
                           writes=[("xin", si % 2)], anchor="pxl%d" % (si % 2))

                def st_norm(si):
                    b_, r_, pos_, nt_ = stl[si]
                    norm_to_hT(xinl[si % 2], ("xin", si % 2), nt_, r_, l, 1, A, xn, junk, ss, rstd, ptr, hTl[si % 2],
                               ("hT", si % 2))

                st_load(0)
                if len(stl) > 1:
                    st_load(1)
                st_norm(0)
                for si, (b, r, pos, ntile) in enumerate(stl):
                    if True:
                        is_ctx = (r == 2)
                        ntok = ntile * 128
                        hT = hTl[si % 2]
                        hk = ("hT", si % 2)
                        gq = []
                        for h0 in range(0, ntok, 512):
                            nn = min(512, ntok - h0)
                            for cc in range(6):
                                gq.append((h0, nn, cc))
                        per_tile = -(-len(gq) // ntile)

                        def emit_gbz():
                            h0, nn, cc = gq.pop(0)
                            for k in range(KC):
                                kb.op("pe", lambda e, k=k: e.matmul(
                                    pz[:, 0:nn], lhsT=wi[:, k, 1024 + cc * 128:1024 + (cc + 1) * 128],
                                    rhs=hT[:, k, h0:h0 + nn], start=(k == 0), stop=(k == KC - 1)),
                                    reads=[(hk, k), "wi"], writes=["pz"])
                            if cc < 2:
                                kb.op("act", lambda e: e.activation(out=gb_sb[:, cc, 0:nn], in_=pz[:, 0:nn], func=AF.Copy),
                                      reads=["pz"], writes=["gb_sb"])
                            elif cc < 4:
                                kb.op("act", lambda e: e.activation(out=gc_sb[:, cc - 2, 0:nn], in_=pz[:, 0:nn], func=AF.Copy),
                                      reads=["pz"], writes=["gc_sb"])
                            else:
                                kb.op("dve", lambda e: e.tensor_tensor(gz_sb[:, cc - 4, 0:nn], gc_sb[:, cc - 4, 0:nn],
                                                                       pz[:, 0:nn], ALU.mult),
                                      reads=["pz", "gc_sb"], writes=["gz_sb"])
                            if cc == 5:
                                kb.dma("sp", gb_d[b, :, :, pos + h0:pos + h0 + nn].rearrange("c p n -> p c n"), gb_sb[:, :, 0:nn],
                                       reads=["gb_sb"], writes=[("gb_d", b, pos + h0)], anchor="gbst")
                                kb.dma("sp", gz_d[b, :, :, pos + h0:pos + h0 + nn].rearrange("c p n -> p c n"), gz_sb[:, :, 0:nn],
                                       reads=["gz_sb"], writes=[("gz_d", b, pos + h0)], anchor="gzst")

                        for t in range(ntile):
                            tp = pos + t * 128
                            for (c0, c1) in ((0, 512), (512, 768)):
                                for k in range(KC):
                                    kb.op("pe", lambda e, k=k, t=t, c0=c0, c1=c1: e.matmul(
                                        pq[:, c0:c1], lhsT=hT[:, k, t * 128:(t + 1) * 128], rhs=wi[:, k, c0:c1],
                                        start=(k == 0), stop=(k == KC - 1)),
                                        reads=[(hk, k), "wi"], writes=["pq"])
                            kb.op("act", lambda e: e.activation(out=v_tm[:], in_=pq[:, 640:768], func=AF.Copy),
                                  reads=["pq"], writes=["v_tm"])
                            kb.dma("sp", v_d[b, tp:tp + 128, :], v_tm[:], reads=["v_tm"], writes=[("v_d", b, tp)],
                                   anchor="vst")
                            for _ in range(per_tile):
                                if gq:
                                    emit_gbz()
                            if is_ctx:
                                kb.op("act", lambda e: e.activation(out=qk_tm[:], in_=pq[:, 0:640], func=AF.Copy),
                                      reads=["pq"], writes=["qk_tm"])
                            else:
                                lt = (tp - NCTX) // 128
                                qv = pq[:, 0:640].rearrange("p (h a s i) -> p h a s i", h=10, a=2, s=2)
                                ov = qk_tm[:, :].rearrange("p (h a s i) -> p h a s i", h=10, a=2, s=2)
                                cosv = rope[:, lt, 0:32].rearrange("p (a i) -> p a i", a=2).unsqueeze(1).to_broadcast([128, 10, 2, 16])
                                sinv = rope[:, lt, 32:64].rearrange("p (a i) -> p a i", a=2).unsqueeze(1).to_broadcast([128, 10, 2, 16])
                                t1 = qv[:, :, :, 0, :]
                                t2 = qv[:, :, :, 1, :]
                                v4 = lambda tl: tl[:, :].rearrange("p (h a i) -> p h a i", h=10, a=2)
                                kb.op("dve", lambda e: e.tensor_tensor(v4(ra), t1, cosv, ALU.mult), reads=["pq", "rope"], writes=["ra"])
                                kb.op("dve", lambda e: e.tensor_tensor(v4(rb), t2, sinv, ALU.mult), reads=["pq", "rope"], writes=["rb"])
                                kb.op("dve", lambda e: e.tensor_tensor(v4(rc), t2, cosv, ALU.mult), reads=["pq", "rope"], writes=["rc"])
                                kb.op("dve", lambda e: e.tensor_tensor(v4(rd), t1, sinv, ALU.mult), reads=["pq", "rope"], writes=["rd"])
                                kb.op("dve", lambda e: e.tensor_tensor(ov[:, :, :, 0, :], v4(ra), v4(rb), ALU.subtract),
                                      reads=["ra", "rb"], writes=["qk_tm"])
                                kb.op("dve", lambda e: e.tensor_tensor(ov[:, :, :, 1, :], v4(rc), v4(rd), ALU.add),
                                      reads=["rc", "rd"], writes=["qk_tm"])
                            for c in range(5):
                                kb.op("pe", lambda e, c=c: e.transpose(pqt[:, c * 128:(c + 1) * 128],
                                                                       qk_tm[:, c * 128:(c + 1) * 128], ident_b[:]),
                                      reads=["qk_tm", "ident_b"], writes=["pqt"])
                            kb.op("act", lambda e: e.activation(out=qkT[:], in_=pqt[:, 0:640].rearrange("p (c n) -> p c n", c=5),
                                                                func=AF.Copy), reads=["pqt"], writes=["qkT"])
                            kb.dma("sp", qT_d[b, :, :, tp:tp + 128].rearrange("q p n -> p q n"), qkT[:, 0:4, :],
                                   reads=["qkT"], writes=[("qT_d", b, tp)], anchor="qst")
                            kb.dma("sp", kT_d[b, :, tp:tp + 128], qkT[:, 4, :], reads=["qkT"], writes=[("kT_d", b, tp)],
                                   anchor="qst")
                        while gq:
                            emit_gbz()
                        if si + 1 < len(stl):
                            st_norm(si + 1)
                        if si + 2 < len(stl):
                            st_load(si + 2)
                        nch = ntok // 8
                        for jh in range(2):
                            for jj in range(4):
                                j = jh * 4 + jj
                                for k in range(KC):
                                    kb.op("pe", lambda e, k=k, j=j, jj=jj: e.matmul(
                                        pu[0:nch, jj * 256:(jj + 1) * 256], lhsT=hT[:, k, j:ntok:8], rhs=wi[:, k, 768:1024],
                                        start=(k == 0), stop=(k == KC - 1)), reads=[(hk, k), "wi"], writes=["pu"])
                            kb.op("act", lambda e, jh=jh: e.activation(
                                out=UT[0:nch, :, jh * 4:(jh + 1) * 4, :].rearrange("p g j h -> p j g h"),
                                in_=pu[0:nch, :].rearrange("p (j g h) -> p j g h", j=4, g=16), func=AF.Copy),
                                reads=["pu"], writes=["UT"])
                        for g in range(16):
                            kb.op("pe", lambda e, g=g: e.transpose(pUT[:, g * 128:g * 128 + nch],
                                                                   UT[0:nch, g, :, :].rearrange("p j h -> p (j h)"),
                                                                   ident_b[0:nch, 0:nch]),
                                  reads=["UT", "ident_b"], writes=["pu"])
                        for gh in range(2):
                            kb.op("dve" if gh == 0 else "act", lambda e, gh=gh: (
                                e.tensor_copy(Usb[:, gh * 8:(gh + 1) * 8, 0:nch],
                                              pUT[:, gh * 1024:(gh + 1) * 1024].rearrange("p (g c) -> p g c", g=8)[:, :, 0:nch])
                                if gh == 0 else
                                e.activation(out=Usb[:, gh * 8:(gh + 1) * 8, 0:nch],
                                             in_=pUT[:, gh * 1024:(gh + 1) * 1024].rearrange("p (g c) -> p g c", g=8)[:, :, 0:nch],
                                             func=AF.Copy)),
                                reads=["pu"], writes=["Usb"])
                        c0 = pos // 8
                        kb.dma("sp", U_d[b, :, :, c0:c0 + nch], Usb[:, :, 0:nch], reads=["Usb"],
                               writes=[("U_d", b, c0)], anchor="ust")
            kb.barrier()

        def ssm_phase(l, ctx_out):
            with ES() as es:
                sm = lambda name, n: S(es, name, [128, n])
                lr = sm("s_lr", 16); li = sm("s_li", 16); ls = sm("s_ls", 16)
                kb.dma("sp", lr[:], lam_re[l, :, :], writes=["s_lr"], anchor="s0")
                kb.dma("sp", li[:], lam_im[l, :, :], writes=["s_li"], anchor="s1")
                kb.dma("sp", ls[:], lstep[l, :, :], writes=["s_ls"], anchor="s2")
                Br = sm("s_Br", 256); Bi = sm("s_Bi", 256); Cr = sm("s_Cr", 256); Ci = sm("s_Ci", 256)
                kb.dma("sp", Br[:], bre_in[l, :, :], writes=["s_Br"], anchor="s3")
                kb.dma("sp", Bi[:], bim_in[l, :, :], writes=["s_Bi"], anchor="s4")
                kb.dma("sp", Cr[:], cre_in[l, :, :], writes=["s_Cr"], anchor="s5")
                kb.dma("sp", Ci[:], cim_in[l, :, :], writes=["s_Ci"], anchor="s6")
                dsk = sm("s_dsk", 16); tmf = sm("s_tmf", 128); tmb = sm("s_tmb", 128); ghm = sm("s_ghm", 2)
                kb.dma("sp", dsk[:], dsk_in[l, :, :], writes=["s_dsk"], anchor="s7")
                kb.dma("sp", tmf[:], tmf_in[:, :], writes=["s_tmf"], anchor="s8")
                kb.dma("sp", tmb[:], tmb_in[:, :], writes=["s_tmb"], anchor="s9")
                kb.dma("sp", ghm[:], ghm_in[:, :], writes=["s_ghm"], anchor="s10")
                bglu = sm("s_bglu", 2)
                kb.dma("sp", bglu[:], bglu_in[l, :, :], writes=["s_bglu"], anchor="s11")
                wgl = S(es, "s_wgl", [128, 2, 256], BF16)
                kb.dma("pool", wgl[:], wglu_in[l, :, :].rearrange("(c p) n -> p c n", p=128), writes=["s_wgl"], anchor="s12")

                cnt = [0]

                def T(n):
                    cnt[0] += 1
                    return sm("s_t%d" % cnt[0], n)

                def dv(fn, reads, writes):
                    kb.op("dve", fn, reads=reads, writes=writes)

                def tt(o, a, b_, op, ks):
                    dv(lambda e: e.tensor_tensor(o, a, b_, op), ks[1:], ks[:1])

                dt_ = T(16); mag = T(16); ang = T(16); tf = T(16); ti = S(es, "s_ti", [128, 16], I32); tk = T(16)
                sn = T(16); cs = T(16)
                kb.op("act", lambda e: e.activation(out=dt_[:], in_=ls[:], func=AF.Exp), reads=["s_ls"], writes=["dt"])
                tt(mag[:], lr[:], dt_[:], ALU.mult, ["mag", "s_lr", "dt"])
                kb.op("act", lambda e: e.activation(out=mag[:], in_=mag[:], func=AF.Exp), reads=["mag"], writes=["mag"])
                tt(ang[:], li[:], dt_[:], ALU.mult, ["ang", "s_li", "dt"])
                for (dst, off) in ((sn, 8.0), (cs, 8.25)):
                    dv(lambda e, off=off: e.tensor_scalar(tf[:], ang[:], 1.0 / TWO_PI, off, ALU.mult, ALU.add), ["ang", "tf"], ["tf"])
                    dv(lambda e: e.tensor_copy(ti[:], tf[:]), ["tf"], ["ti"])
                    dv(lambda e: e.tensor_copy(tk[:], ti[:]), ["ti"], ["tk"])
                    tt(tf[:], tf[:], tk[:], ALU.subtract, ["tf", "tf", "tk"])
                    kb.op("act", lambda e, dst=dst: e.activation(out=dst[:], in_=tf[:], func=AF.Sin, scale=TWO_PI),
                          reads=["tf"], writes=["sc"])
                Er = S(es, "s_Er", [128, 9, 16]); Ei = S(es, "s_Ei", [128, 9, 16])
                dv(lambda e: e.memset(Er[:, 0, :], 1.0), [], ["E"])
                dv(lambda e: e.memset(Ei[:, 0, :], 0.0), ["E"], ["E"])
                tt(Er[:, 1, :], mag[:], cs[:], ALU.mult, ["E", "mag", "sc"])
                tt(Ei[:, 1, :], mag[:], sn[:], ALU.mult, ["E", "mag", "sc"])
                t1 = T(16); t2 = T(16)

                def cmul(orr, oi, ar_, ai_, br_, bi_, keyo, keys):
                    tt(t1[:], ar_, br_, ALU.mult, ["t1"] + keys)
                    tt(t2[:], ai_, bi_, ALU.mult, ["t2"] + keys)
                    tt(orr, t1[:], t2[:], ALU.subtract, [keyo, "t1", "t2"])
                    tt(t1[:], ar_, bi_, ALU.mult, ["t1"] + keys)
                    tt(t2[:], ai_, br_, ALU.mult, ["t2"] + keys)
                    tt(oi, t1[:], t2[:], ALU.add, [keyo, "t1", "t2"])

                for m in range(2, 9):
                    cmul(Er[:, m, :], Ei[:, m, :], Er[:, m - 1, :], Ei[:, m - 1, :], Er[:, 1, :], Ei[:, 1, :], "E", ["E"])
                Pr = S(es, "s_Pr", [128, nlev, 16]); Pi = S(es, "s_Pi", [128, nlev, 16]); Pn = S(es, "s_Pn", [128, nlev, 16])
                dv(lambda e: e.tensor_copy(Pr[:, 0, :], Er[:, 8, :]), ["E"], ["P"])
                dv(lambda e: e.tensor_copy(Pi[:, 0, :], Ei[:, 8, :]), ["E", "P"], ["P"])
                for k in range(1, nlev):
                    cmul(Pr[:, k, :], Pi[:, k, :], Pr[:, k - 1, :], Pi[:, k - 1, :], Pr[:, k - 1, :], Pi[:, k - 1, :], "P", ["P"])
                dv(lambda e: e.tensor_scalar(Pn[:], Pi[:], -1.0, None, ALU.mult), ["P"], ["Pn"])
                i8r = T(16); i8i = T(16); dn = T(16)
                tt(t1[:], Er[:, 8, :], Er[:, 8, :], ALU.mult, ["t1", "E"])
                tt(t2[:], Ei[:, 8, :], Ei[:, 8, :], ALU.mult, ["t2", "E"])
                tt(dn[:], t1[:], t2[:], ALU.add, ["dn", "t1", "t2"])
                dv(lambda e: e.reciprocal(dn[:], dn[:]), ["dn"], ["dn"])
                tt(i8r[:], Er[:, 8, :], dn[:], ALU.mult, ["i8", "E", "dn"])
                dv(lambda e: e.scalar_tensor_tensor(i8i[:], Ei[:, 8, :], -1.0, dn[:], ALU.mult, ALU.mult), ["E", "dn", "i8"], ["i8"])
                gr = T(16); gi = T(16); am1 = T(16); t3 = T(16)
                dv(lambda e: e.tensor_scalar(am1[:], Er[:, 1, :], -1.0, None, ALU.add), ["E"], ["am1"])
                tt(t1[:], lr[:], lr[:], ALU.mult, ["t1", "s_lr"])
                tt(t2[:], li[:], li[:], ALU.mult, ["t2", "s_li"])
                tt(dn[:], t1[:], t2[:], ALU.add, ["dn", "t1", "t2", "i8"])
                dv(lambda e: e.reciprocal(dn[:], dn[:]), ["dn"], ["dn"])
                tt(t1[:], am1[:], lr[:], ALU.mult, ["t1", "am1", "s_lr"])
                tt(t2[:], Ei[:, 1, :], li[:], ALU.mult, ["t2", "E", "s_li"])
                tt(t3[:], t1[:], t2[:], ALU.add, ["t3", "t1", "t2"])
                tt(gr[:], t3[:], dn[:], ALU.mult, ["g", "t3", "dn"])
                tt(t1[:], Ei[:, 1, :], lr[:], ALU.mult, ["t1", "E", "s_lr"])
                tt(t2[:], am1[:], li[:], ALU.mult, ["t2", "am1", "s_li"])
                tt(t3[:], t1[:], t2[:], ALU.subtract, ["t3", "t1", "t2"])
                tt(gi[:], t3[:], dn[:], ALU.mult, ["g", "t3", "dn", "g"])
                bbr = T(256); bbi = T(256); u1 = T(256); u2 = T(256)
                v3 = lambda tl: tl[:, :].rearrange("p (a h) -> p a h", h=16)
                bc16 = lambda ap: ap.unsqueeze(2).to_broadcast([128, 16, 16])
                tt(v3(u1), v3(Br), bc16(gr[:]), ALU.mult, ["u1", "s_Br", "g"])
                tt(v3(u2), v3(Bi), bc16(gi[:]), ALU.mult, ["u2", "s_Bi", "g"])
                tt(bbr[:], u1[:], u2[:], ALU.subtract, ["bb", "u1", "u2"])
                tt(v3(u1), v3(Bi), bc16(gr[:]), ALU.mult, ["u1", "s_Bi", "g"])
                tt(v3(u2), v3(Br), bc16(gi[:]), ALU.mult, ["u2", "s_Br", "g"])
                tt(bbi[:], u1[:], u2[:], ALU.add, ["bb", "u1", "u2", "bb"])
                Toep = S(es, "s_Toep", [128, 2, 16, 128], BF16)
                WcR = S(es, "s_WcR", [128, 2, 16, 128], BF16)
                WcI = S(es, "s_WcI", [128, 2, 16, 128], BF16)
                Wb = S(es, "s_Wb", [128, 2, 2, 8, 128], BF16)
                est = ES()
                XTr = S(est, "s_XTr", [128, 2, 8, 128]); XTi = S(est, "s_XTi", [128, 2, 8, 128])
                Yr = S(est, "s_Yr", [128, 2, 8, 128]); Yi = S(est, "s_Yi", [128, 2, 8, 128])
                w1 = S(est, "s_w1", [128, 8, 16]); w2 = S(est, "s_w2", [128, 8, 16])
                v3d = lambda tl, d: tl[:, d * 128:(d + 1) * 128].rearrange("p (q h) -> p q h", h=16)
                bc8 = lambda ap: ap.unsqueeze(2).to_broadcast([128, 8, 16])
                for d in range(2):
                    for i in range(8):
                        mx = (7 - i) if d == 0 else i
                        my = (i + 1) if d == 0 else (8 - i)
                        er_x = bc8(Er[:, mx, d * 8:(d + 1) * 8]); ei_x = bc8(Ei[:, mx, d * 8:(d + 1) * 8])
                        er_y = bc8(Er[:, my, d * 8:(d + 1) * 8]); ei_y = bc8(Ei[:, my, d * 8:(d + 1) * 8])
                        ox_r = XTr[:, d, :, i * 16:(i + 1) * 16]; ox_i = XTi[:, d, :, i * 16:(i + 1) * 16]
                        oy_r = Yr[:, d, :, i * 16:(i + 1) * 16]; oy_i = Yi[:, d, :, i * 16:(i + 1) * 16]
                        for (o_r, o_i, a_r, a_i, e_r, e_i, ko, ka) in (
                                (ox_r, ox_i, v3d(bbr, d), v3d(bbi, d), er_x, ei_x, "XT", "bb"),
                                (oy_r, oy_i, v3d(Cr, d), v3d(Ci, d), er_y, ei_y, "Y", "s_Cr")):
                            kin = [ka, "E", "s_Ci"]
                            tt(w1[:], a_r, e_r, ALU.mult, ["w1"] + kin)
                            tt(w2[:], a_i, e_i, ALU.mult, ["w2"] + kin)
                            tt(o_r, w1[:], w2[:], ALU.subtract, [ko, "w1", "w2"])
                            tt(w1[:], a_r, e_i, ALU.mult, ["w1"] + kin)
                            tt(w2[:], a_i, e_r, ALU.mult, ["w2"] + kin)
                            tt(o_i, w1[:], w2[:], ALU.add, [ko, "w1", "w2"])
                Ypr = S(est, "s_Ypr", [128, 2, 8, 128]); Ypi = S(est, "s_Ypi", [128, 2, 8, 128])
                z1 = S(est, "s_z1", [128, 16, 128]); z2 = S(est, "s_z2", [128, 16, 128])
                f16 = lambda tl: tl[:, :, :, :].rearrange("p d q n -> p (d q) n")
                bcn = lambda ap: ap.unsqueeze(2).to_broadcast([128, 16, 128])
                tt(z1[:], f16(Yr), bcn(i8r[:]), ALU.mult, ["z1", "Y", "i8"])
                tt(z2[:], f16(Yi), bcn(i8i[:]), ALU.mult, ["z2", "Y", "i8"])
                tt(f16(Ypr), z1[:], z2[:], ALU.subtract, ["Yp", "z1", "z2"])
                tt(z1[:], f16(Yr), bcn(i8i[:]), ALU.mult, ["z1", "Y", "i8"])
                tt(z2[:], f16(Yi), bcn(i8r[:]), ALU.mult, ["z2", "Y", "i8"])
                dv(lambda e: e.scalar_tensor_tensor(f16(Ypi), z1[:], -1.0, z2[:], ALU.mult, ALU.subtract), ["z1", "z2", "Yp"], ["Yp"])
                yzr = [S(est, "s_yzr%d" % i, [128, 128]) for i in range(2)]
                yzi = [S(est, "s_yzi%d" % i, [128, 128]) for i in range(2)]
                tsb = [S(est, "s_tsb%d" % i, [128, 128]) for i in range(2)]
                with ES() as es2:
                    pt = [PS(es2, "s_pt%d" % i, [128, 512]) for i in range(2)]
                    pw = [PS(es2, "s_pw%d" % i, [128, 512]) for i in range(2)]
                    n = 0
                    for d in range(2):
                        for g in range(16):
                            gh, q = g // 8, g % 8
                            sl = n % 2
                            n += 1
                            dv(lambda e, sl=sl, d=d, q=q, gh=gh: e.tensor_scalar(yzr[sl][:], Ypr[:, d, q, :], ghm[:, gh:gh + 1], None, ALU.mult),
                               ["Yp", "s_ghm", ("yzr", sl)], [("yzr", sl)])
                            dv(lambda e, sl=sl, d=d, q=q, gh=gh: e.tensor_scalar(yzi[sl][:], Ypi[:, d, q, :], ghm[:, gh:gh + 1], None, ALU.mult),
                               ["Yp", "s_ghm", ("yzi", sl)], [("yzi", sl)])
                            kb.op("pe", lambda e, sl=sl, d=d, q=q: e.matmul(pt[sl][:, 0:128], lhsT=XTr[:, d, q, :], rhs=yzr[sl][:],
                                                                       start=True, stop=False), reads=["XT", ("yzr", sl)], writes=[("pt", sl)])
                            kb.op("pe", lambda e, sl=sl, d=d, q=q: e.matmul(pt[sl][:, 0:128], lhsT=XTi[:, d, q, :], rhs=yzi[sl][:],
                                                                       start=False, stop=True), reads=["XT", ("yzi", sl)], writes=[("pt", sl)])
                            msk = tmf if d == 0 else tmb
                            if d == 0:
                                dv(lambda e, sl=sl, msk=msk: e.tensor_tensor(tsb[sl][:], pt[sl][:, 0:128], msk[:], ALU.mult),
                                   [("pt", sl), "s_tmf", ("tsb", sl)], [("tsb", sl)])
                                dv(lambda e, sl=sl, g=g, d=d: e.scalar_tensor_tensor(Toep[:, d, g, :], ident_f[:], dsk[:, g:g + 1], tsb[sl][:],
                                                                                 ALU.mult, ALU.add),
                                   [("tsb", sl), "ident_f", "s_dsk"], ["Toep"])
                            else:
                                dv(lambda e, sl=sl, msk=msk, g=g, d=d: e.tensor_tensor(Toep[:, d, g, :], pt[sl][:, 0:128], msk[:], ALU.mult),
                                   [("pt", sl), "s_tmb"], ["Toep"])
                            dv(lambda e, d=d, q=q, g=g, gh=gh: e.tensor_scalar(WcR[:, d, g, :], Yr[:, d, q, :], ghm[:, gh:gh + 1], None, ALU.mult),
                               ["Y", "s_ghm"], ["WcR"])
                            dv(lambda e, d=d, q=q, g=g, gh=gh: e.tensor_scalar(WcI[:, d, g, :], Yi[:, d, q, :], ghm[:, gh:gh + 1], -1.0, ALU.mult, ALU.mult),
                               ["Y", "s_ghm"], ["WcI"])
                    n = 0
                    for d in range(2):
                        for comp, XT in ((0, XTr), (1, XTi)):
                            for q in range(8):
                                sl = n % 2
                                n += 1
                                kb.op("pe", lambda e, sl=sl, d=d, q=q, XT=XT: e.transpose(pw[sl][:, 0:128], XT[:, d, q, :], ident_f[:]),
                                      reads=["XT", "ident_f"], writes=[("pw", sl)])
                                kb.op("act", lambda e, sl=sl, d=d, comp=comp, q=q: e.activation(out=Wb[:, d, comp, q, :], in_=pw[sl][:, 0:128], func=AF.Copy),
                                      reads=[("pw", sl)], writes=["Wb"])
                est.close()
                kb.barrier()
                U = S(es, "s_U", [128, 16, NCH], BF16)
                SR = S(es, "s_SR", [128, 8, NCH]); SI = S(es, "s_SI", [128, 8, NCH])
                TR = S(es, "s_TR", [128, 8, NCH]); TI = S(es, "s_TI", [128, 8, NCH])
                Sb = S(es, "s_Sb", [128, 2, 2, 8, NCH + 1], BF16)
                G = S(es, "s_G", [128, 8, 256], BF16)
                GT = S(es, "s_GT", [128, 2, 1024], BF16)
                ya = S(es, "s_ya", [128, 2048]); yb = S(es, "s_yb", [128, 2048])
                sgm = S(es, "s_sgm", [128, 2, 512])
                so = S(es, "s_so", [128, 2, 1024], BF16)
                with ES() as es2:
                    pzr = PS(es2, "s_pzr", [128, 512]); pzi = PS(es2, "s_pzi", [128, 512])
                    py = PS(es2, "s_py", [128, 2048])
                    pgt = PS(es2, "s_pgt", [128, 1024], BF16)
                    pgl = PS(es2, "s_pgl", [128, 512])
                    for b in range(NB):
                        kb.dma("sp", U[:], U_d[b, :, :, :], reads=[], writes=["U"], anchor="ul")
                        for d in range(2):
                            if d == 0:
                                segl = [(0, NCC, 0), (NCC, NLC, NCC)]
                            else:
                                segl = [(NCC, NLC, 0), (0, NCC, NLC)]
                            for q in range(8):
                                for (s0, n_, d0) in segl:
                                    for c0 in range(0, n_, 512):
                                        nn = min(512, n_ - c0)
                                        for comp, pz_, dstS in ((0, pzr, SR), (1, pzi, SI)):
                                            for gh in range(2):
                                                kb.op("pe", lambda e, comp=comp, gh=gh, pz_=pz_, s0=s0, c0=c0, nn=nn, d=d, q=q: e.matmul(
                                                    pz_[gh * 64:(gh + 1) * 64, 0:nn], lhsT=Wb[:, d, comp, q, gh * 64:(gh + 1) * 64],
                                                    rhs=U[:, gh * 8 + q, s0 + c0:s0 + c0 + nn], start=True, stop=True),
                                                    reads=["Wb", "U"], writes=["pz%d" % comp])
                                            kb.op("act", lambda e, pz_=pz_, dstS=dstS, q=q, d0=d0, c0=c0, nn=nn: e.activation(
                                                out=dstS[:, q, d0 + c0:d0 + c0 + nn], in_=pz_[:, 0:nn], func=AF.Copy),
                                                reads=["pz%d" % comp], writes=[("S", q)])
                            cur = (SR, SI); nxt = (TR, TI)
                            ck = "S"; nk = "T"
                            for k in range(nlev):
                                sh = 1 << k
                                if sh >= NCH:
                                    break
                                cr_, ci_ = cur[0], cur[1]
                                nr_, ni_ = nxt[0], nxt[1]
                                if d == 0:
                                    o_sl = slice(sh, NCH); s_sl = slice(0, NCH - sh); keep = slice(0, sh)
                                else:
                                    o_sl = slice(0, NCH - sh); s_sl = slice(sh, NCH); keep = slice(NCH - sh, NCH)
                                for q in range(8):
                                    r0 = kb._r((nk, q))
                                    base = dict(r0[0])
                                    for sname, v in r0[1].items():
                                        if base.get(sname, 0) < v:
                                            base[sname] = v
                                    for j in range(4):
                                        kb.res[(nk, q, j)] = [dict(base), {}]
                                for q in range(8):
                                    col = d * 8 + q
                                    pr_ = Pr[:, k, col:col + 1]
                                    rk = [(ck, q), "P", "Pn"]
                                    dv(lambda e, q=q, pr_=pr_: e.scalar_tensor_tensor(
                                        nr_[:, q, o_sl], cr_[:, q, s_sl], pr_, cr_[:, q, o_sl], ALU.mult, ALU.add), rk + [(nk, q, 0)], [(nk, q, 0)])
                                    dv(lambda e, q=q, pr_=pr_: e.scalar_tensor_tensor(
                                        ni_[:, q, o_sl], ci_[:, q, s_sl], pr_, ci_[:, q, o_sl], ALU.mult, ALU.add), rk + [(nk, q, 1)], [(nk, q, 1)])
                                    kb.op("act", lambda e, q=q: e.activation(out=nr_[:, q, keep], in_=cr_[:, q, keep], func=AF.Copy),
                                          reads=[(ck, q)], writes=[(nk, q, 2)])
                                    kb.op("act", lambda e, q=q: e.activation(out=ni_[:, q, keep], in_=ci_[:, q, keep], func=AF.Copy),
                                          reads=[(ck, q)], writes=[(nk, q, 3)])
                                for q in range(8):
                                    col = d * 8 + q
                                    pi_ = Pi[:, k, col:col + 1]; pn_ = Pn[:, k, col:col + 1]
                                    rk = [(ck, q), "P", "Pn"]
                                    dv(lambda e, q=q, pn_=pn_: e.scalar_tensor_tensor(
                                        nr_[:, q, o_sl], ci_[:, q, s_sl], pn_, nr_[:, q, o_sl], ALU.mult, ALU.add), rk + [(nk, q, 0)], [(nk, q, 0)])
                                    dv(lambda e, q=q, pi_=pi_: e.scalar_tensor_tensor(
                                        ni_[:, q, o_sl], cr_[:, q, s_sl], pi_, ni_[:, q, o_sl], ALU.mult, ALU.add), rk + [(nk, q, 1)], [(nk, q, 1)])
                                for q in range(8):
                                    r0 = kb._r((nk, q))
                                    r0[0] = {}
                                    r0[1] = {}
                                    for j in range(4):
                                        for sname, v in kb._r((nk, q, j))[0].items():
                                            if r0[0].get(sname, 0) < v:
                                                r0[0][sname] = v
                                cur, nxt = nxt, cur
                                ck, nk = nk, ck
                            for comp in range(2):
                                src = cur[comp]
                                if d == 0:
                                    kb.op("pool", lambda e, comp=comp: e.memset(Sb[:, d, comp, :, 0:1], 0.0), reads=[], writes=[("Sb", d, comp)])
                                    kb.op("act", lambda e, comp=comp, src=src: e.activation(out=Sb[:, 0, comp, :, 1:NCH + 1], in_=src[:, :, 0:NCH], func=AF.Copy),
                                          reads=[(ck, q) for q in range(8)], writes=[("Sb", d, comp)])
                                else:
                                    kb.op("pool", lambda e, comp=comp: e.memset(Sb[:, 1, comp, :, NCH - 1:NCH + 1], 0.0), reads=[], writes=[("Sb", d, comp)])
                                    kb.op("act", lambda e, comp=comp, src=src: e.activation(out=Sb[:, 1, comp, :, 0:NCH - 1], in_=src[:, :, 1:NCH], func=AF.Copy),
                                          reads=[(ck, q) for q in range(8)], writes=[("Sb", d, comp)])
                        blocks = []
                        if ctx_out:
                            blocks.append((0, NCC))
                        for c0 in range(0, NLC, 128):
                            blocks.append((NCC + c0, 128))
                        sbk = [("Sb", dd, cc) for dd in range(2) for cc in range(2)]
                        for (m0, nb_) in blocks:
                            is_ctx = m0 < NCC
                            for g in range(16):
                                q = g % 8
                                mb0 = (m0 - NCC) if not is_ctx else (NLC + m0)
                                ops_ = [(U[:, g, m0:m0 + nb_], Toep[:, 0, g, :]),
                                        (Sb[:, 0, 0, q, m0:m0 + nb_], WcR[:, 0, g, :]),
                                        (Sb[:, 0, 1, q, m0:m0 + nb_], WcI[:, 0, g, :]),
                                        (U[:, g, m0:m0 + nb_], Toep[:, 1, g, :]),
                                        (Sb[:, 1, 0, q, mb0:mb0 + nb_], WcR[:, 1, g, :]),
                                        (Sb[:, 1, 1, q, mb0:mb0 + nb_], WcI[:, 1, g, :])]
                                for oi, (lh, rh) in enumerate(ops_):
                                    kb.op("pe", lambda e, lh=lh, rh=rh, oi=oi, g=g: e.matmul(
                                        py[0:nb_, g * 128:(g + 1) * 128], lhsT=lh, rhs=rh, start=(oi == 0), stop=(oi == 5)),
                                        reads=["U", "Toep", "WcR", "WcI"] + sbk, writes=["py"])
                            kb.op("act", lambda e: e.activation(out=ya[0:nb_, :], in_=py[0:nb_, :], func=AF.Square), reads=["py"], writes=["ya"])
                            dv(lambda e: e.tensor_scalar(ya[0:nb_, :], ya[0:nb_, :], 0.044715, 1.0, ALU.mult, ALU.add), ["ya"], ["ya"])
                            dv(lambda e: e.tensor_tensor(ya[0:nb_, :], ya[0:nb_, :], py[0:nb_, :], ALU.mult), ["ya", "py"], ["ya"])
                            kb.op("act", lambda e: e.activation(out=yb[0:nb_, :], in_=ya[0:nb_, :], func=AF.Sigmoid, scale=1.5957691216057308),
                                  reads=["ya"], writes=["yb"])
                            dv(lambda e: e.tensor_tensor(G[0:nb_, :, :].rearrange("p j (g h) -> p g j h", g=16),
                                                         yb[0:nb_, :].rearrange("p (g j h) -> p g j h", g=16, j=8),
                                                         py[0:nb_, :].rearrange("p (g j h) -> p g j h", g=16, j=8), ALU.mult),
                               ["yb", "py"], ["G"])
                            ntok = nb_ * 8
                            for hf in range(2):
                                for j in range(8):
                                    kb.op("pe", lambda e, hf=hf, j=j: e.transpose(pgt[:, j * 128:j * 128 + nb_], G[0:nb_, j, hf * 128:(hf + 1) * 128],
                                                                                  ident_b[0:nb_, 0:nb_]), reads=["G", "ident_b"], writes=["pgt"])
                                kb.op("act", lambda e, hf=hf: e.activation(
                                    out=GT[:, hf, 0:ntok].rearrange("p (c j) -> p j c", j=8),
                                    in_=pgt[:, :].rearrange("p (j c) -> p j c", j=8)[:, :, 0:nb_], func=AF.Copy),
                                    reads=["pgt"], writes=["GT"])
                            for t0 in range(0, ntok, 512):
                                nn = min(512, ntok - t0)
                                for ho in range(2):
                                    for hi in range(2):
                                        kb.op("pe", lambda e, ho=ho, hi=hi, t0=t0, nn=nn: e.matmul(
                                            pgl[:, 0:nn], lhsT=wgl[:, hi, ho * 128:(ho + 1) * 128], rhs=GT[:, hi, t0:t0 + nn],
                                            start=(hi == 0), stop=(hi == 1)), reads=["s_wgl", "GT"], writes=["pgl"])
                                    kb.op("act", lambda e, ho=ho, nn=nn: e.activation(out=sgm[:, ho, 0:nn], in_=pgl[:, 0:nn], func=AF.Sigmoid,
                                                                                      bias=bglu[:, ho:ho + 1]), reads=["pgl", "s_bglu"], writes=["sgm"])
                                    dv(lambda e, ho=ho, t0=t0, nn=nn: e.tensor_tensor(so[:, ho, t0:t0 + nn], sgm[:, ho, 0:nn], GT[:, ho, t0:t0 + nn], ALU.mult),
                                       ["sgm", "GT"], ["so"])
                            p0 = m0 * 8
                            kb.dma("sp", mix_d[b, 4:6, :, p0:p0 + ntok].rearrange("c p n -> p c n"), so[:, :, 0:ntok],
                                   reads=["so"], writes=[("mix_d", b, "ssm", p0)], anchor="sst")
            kb.barrier()

        def attn_phase(l, ctx_out):
            with ES() as es:
                qT = S(es, "a_qT", [128, 4, NP], BF16)
                kTe = S(es, "a_kTe", [128, 2, NP], BF16)
                kTo = S(es, "a_kTo", [128, 2, NP], BF16)
                v2e = S(es, "a_v2e", [128, NKC, 2, 128], BF16)
                v2o = S(es, "a_v2o", [128, NKC, 2, 128], BF16)
                onesE = S(es, "a_onesE", [128, 128], BF16)
                onesO = S(es, "a_onesO", [128, 128], BF16)
                mprev = S(es, "a_mprev", [128, 128], BF16); mnext = S(es, "a_mnext", [128, 128], BF16)
                mf32 = S(es, "a_mf32", [128, 256])
                esk = S(es, "a_esk", [128, 8])
                esb = S(es, "a_esb", [128, 2, 256])
                pT = [S(es, "a_pT%d" % i, [128, 512], BF16) for i in range(4)]
                dsb = S(es, "a_dsb", [128, 256])
                ao = [S(es, "a_ao%d" % i, [128, 2, 128], BF16) for i in range(2)]
                cw = S(es, "a_cw", [128, 6])
                gz = S(es, "a_gz", [128, 2, NP + 4], BF16)
                gb = S(es, "a_gb", [128, 2, NP], BF16)
                ctmp = S(es, "a_ctmp", [128, L])
                cvo = S(es, "a_cvo", [128, 2, NP], BF16)
                kb.op("dve", lambda e: e.memset(onesE[:], 0.0), writes=["a_ones"])
                kb.op("dve", lambda e: e.memset(onesO[:], 0.0), writes=["a_ones"])
                kb.op("dve", lambda e: e.memset(onesE[:, 0:64], 1.0), reads=["a_ones"], writes=["a_ones"])
                kb.op("dve", lambda e: e.memset(onesO[:, 64:128], 1.0), reads=["a_ones"], writes=["a_ones"])
                kb.op("dve", lambda e: e.memset(v2e[:, :, :, 64:128], 0.0), writes=["a_v2z"])
                kb.op("dve", lambda e: e.memset(v2o[:, :, :, 0:64], 0.0), writes=["a_v2z"])
                kb.op("dve", lambda e: e.memset(kTe[64:128, :, :], 0.0), writes=["kTe_z"])
                kb.op("dve", lambda e: e.memset(kTo[0:64, :, :], 0.0), writes=["kTo_z"])
                kb.dma("sp", mf32[:, 0:128], mprev_in[:, :], writes=["a_mf32"], anchor="a0")
                kb.dma("sp", mf32[:, 128:256], mnext_in[:, :], writes=["a_mf32"], anchor="a0")
                kb.op("dve", lambda e: e.tensor_copy(mprev[:], mf32[:, 0:128]), reads=["a_mf32"], writes=["a_mprev"])
                kb.op("dve", lambda e: e.tensor_copy(mnext[:], mf32[:, 128:256]), reads=["a_mf32"], writes=["a_mnext"])
                kb.dma("sp", esk[:], sink_in[l, :, :], writes=["a_esk"], anchor="a1")
                kb.op("act", lambda e: e.activation(out=esk[:], in_=esk[:], func=AF.Exp), reads=["a_esk"], writes=["a_esk"])
                for kh in range(2):
                    for pp in range(2):
                        for par in range(2):
                            hd = 4 * kh + 2 * pp + par
                            kb.op("dve", lambda e, kh=kh, pp=pp, par=par, hd=hd: e.tensor_copy(
                                esb[par * 64:(par + 1) * 64, kh, pp * 128:(pp + 1) * 128],
                                esk[par * 64:(par + 1) * 64, hd:hd + 1].to_broadcast([64, 128])),
                                reads=["a_esk"], writes=["a_esb"])
                kb.dma("sp", cw[:], convw_in[l, :, :], writes=["a_cw"], anchor="a2")
                with ES() as es2:
                    pst = [PS(es2, "a_pst%d" % i, [128, 512]) for i in range(3)]
                    pnum = [PS(es2, "a_pnum%d" % i, [128, 512]) for i in range(2)]
                    pden = [PS(es2, "a_pden%d" % i, [128, 512]) for i in range(2)]
                    pst = pst + [PS(es2, "a_pst3", [128, 512])]
                    for b in range(NB):
                        kb.dma("sp", qT[:], qT_d[b, :, :, :].rearrange("q p n -> p q n"), writes=["a_qT"], anchor="a3")
                        for kh in range(2):
                            kb.dma("sp", kTe[0:64, kh, :], kT_d[b, kh * 64:(kh + 1) * 64, :], reads=["kTe_z"], writes=["a_kT"], anchor="a4")
                            kb.dma("sp", kTo[64:128, kh, :], kT_d[b, kh * 64:(kh + 1) * 64, :], reads=["kTo_z"], writes=["a_kT"], anchor="a4")
                            for dup, vt in ((0, v2e), (1, v2o)):
                                kb.dma("sp", vt[:, :, kh, dup * 64:(dup + 1) * 64],
                                       v_d[b, :, kh * 64:(kh + 1) * 64].rearrange("(c p) d -> p c d", p=128),
                                       reads=["a_v2z"], writes=["a_v2"], anchor="a5")
                        qblocks = []
                        if ctx_out:
                            for n in range(NCTX // 128):
                                qblocks.append((n * 128, [(0, None), (1, None)]))
                        ncl = NCTX // 128
                        for n in range(NT):
                            kl = [(0, None), (1, None)]
                            if n >= 1:
                                kl.append((ncl + n - 1, mprev))
                            kl.append((ncl + n, None))
                            if n + 1 < NT:
                                kl.append((ncl + n + 1, mnext))
                            qblocks.append((NCTX + n * 128, kl))
                        items = []
                        it = 0
                        for (qp, kl) in qblocks:
                            for kh in range(2):
                                ns = it % 2
                                it += 1
                                for ki, (kc, msk) in enumerate(kl):
                                    items.append((qp, kh, ns, ki, kc, msk, len(kl)))

                        def emit_qk(idx):
                            qp, kh, ns, ki, kc, msk, nk = items[idx]
                            sl = idx % 4
                            for par, kt in ((0, kTe), (1, kTo)):
                                kb.op("pe", lambda e, par=par, kt=kt: e.matmul(
                                    pst[sl][:, par * 256:(par + 1) * 256], lhsT=kt[:, kh, kc * 128:(kc + 1) * 128],
                                    rhs=qT[:, 2 * kh:2 * kh + 2, qp:qp + 128], start=True, stop=True),
                                    reads=["a_kT", "a_qT"], writes=[("pst", sl)])
                            kb.op("act", lambda e: e.activation(out=pT[sl][:], in_=pst[sl][:], func=AF.Exp, scale=0.125),
                                  reads=[("pst", sl)], writes=[("pT", sl)])
                            if msk is not None:
                                kb.op("dve", lambda e: e.tensor_tensor(
                                    pT[sl][:, :].rearrange("p (h q) -> p h q", h=4), pT[sl][:, :].rearrange("p (h q) -> p h q", h=4),
                                    msk[:, :].unsqueeze(1).to_broadcast([128, 4, 128]), ALU.mult),
                                    reads=[("pT", sl), "a_mprev", "a_mnext"], writes=[("pT", sl)])

                        def emit_pv(idx):
                            qp, kh, ns, ki, kc, msk, nk = items[idx]
                            sl = idx % 4
                            for par, vt in ((0, v2e), (1, v2o)):
                                kb.op("pe", lambda e, par=par, vt=vt: e.matmul(
                                    pnum[ns][:, 0:256], lhsT=vt[:, kc, kh, :], rhs=pT[sl][:, par * 256:(par + 1) * 256],
                                    start=(ki == 0 and par == 0), stop=(ki == nk - 1 and par == 1)),
                                    reads=["a_v2", ("pT", sl)], writes=[("pnum", ns)])
                            for par, on in ((0, onesE), (1, onesO)):
                                kb.op("pe", lambda e, par=par, on=on: e.matmul(
                                    pden[ns][:, 0:256], lhsT=on[:], rhs=pT[sl][:, par * 256:(par + 1) * 256],
                                    start=(ki == 0 and par == 0), stop=(ki == nk - 1 and par == 1)),
                                    reads=["a_ones", ("pT", sl)], writes=[("pden", ns)])
                            if ki == nk - 1:
                                kb.op("dve", lambda e: e.tensor_tensor(dsb[:], pden[ns][:, 0:256], esb[:, kh, :], ALU.add),
                                      reads=[("pden", ns), "a_esb"], writes=["a_dsb"])
                                kb.op("dve", lambda e: e.reciprocal(dsb[:], dsb[:]), reads=["a_dsb"], writes=["a_dsb"])
                                kb.op("dve", lambda e: e.tensor_tensor(ao[ns][:, :, :].rearrange("p h q -> p (h q)"), pnum[ns][:, 0:256],
                                                                       dsb[:], ALU.mult),
                                      reads=[("pnum", ns), "a_dsb"], writes=[("ao", ns)])
                                kb.dma("sp", mix_d[b, 2 * kh:2 * kh + 2, :, qp:qp + 128].rearrange("c p n -> p c n"), ao[ns][:],
                                       reads=[("ao", ns)], writes=[("mix_d", b, "att", kh, qp)], anchor="ao%d" % ns)

                        LAG = 3
                        for idx in range(len(items)):
                            emit_qk(idx)
                            if idx >= LAG:
                                emit_pv(idx - LAG)
                        for idx in range(max(0, len(items) - LAG), len(items)):
                            emit_pv(idx)
                        kb.op("dve", lambda e: e.memset(gz[:, :, 0:1], 0.0), writes=["a_gzp"])
                        kb.op("dve", lambda e: e.memset(gz[:, :, NCTX + 1:NCTX + 3], 0.0), writes=["a_gzp"])
                        kb.op("dve", lambda e: e.memset(gz[:, :, NP + 3:NP + 4], 0.0), writes=["a_gzp"])
                        kb.dma("sp", gz[:, :, 1:NCTX + 1], gz_d[b, :, :, 0:NCTX].rearrange("c p n -> p c n"), reads=["a_gzp"], writes=["a_gz"], anchor="a6")
                        kb.dma("sp", gz[:, :, NCTX + 3:NP + 3], gz_d[b, :, :, NCTX:NP].rearrange("c p n -> p c n"), reads=["a_gzp"], writes=["a_gz"], anchor="a6")
                        kb.dma("sp", gb[:], gb_d[b, :, :, :].rearrange("c p n -> p c n"), writes=["a_gb"], anchor="a7")
                        sgl = [(NCTX + 3, NCTX, L)]
                        if ctx_out:
                            sgl.append((1, 0, NCTX))
                        for (g0, o0, n_) in sgl:
                            for hf in range(2):
                                kb.op("dve", lambda e, hf=hf, g0=g0, n_=n_: e.tensor_scalar(ctmp[:, 0:n_], gz[:, hf, g0 - 1:g0 - 1 + n_], cw[:, hf * 3:hf * 3 + 1], None, ALU.mult),
                                      reads=["a_gz", "a_gzp", "a_cw"], writes=["a_ctmp"])
                                for kk in (1, 2):
                                    kb.op("dve", lambda e, hf=hf, g0=g0, n_=n_, kk=kk: e.scalar_tensor_tensor(
                                        ctmp[:, 0:n_], gz[:, hf, g0 - 1 + kk:g0 - 1 + kk + n_], cw[:, hf * 3 + kk:hf * 3 + kk + 1], ctmp[:, 0:n_], ALU.mult, ALU.add),
                                        reads=["a_gz", "a_gzp", "a_cw", "a_ctmp"], writes=["a_ctmp"])
                                kb.op("dve", lambda e, hf=hf, o0=o0, n_=n_: e.tensor_tensor(cvo[:, hf, o0:o0 + n_], ctmp[:, 0:n_], gb[:, hf, o0:o0 + n_], ALU.mult),
                                      reads=["a_ctmp", "a_gb"], writes=["a_cvo"])
                            kb.dma("sp", mix_d[b, 6:8, :, o0:o0 + n_].rearrange("c p n -> p c n"), cvo[:, :, o0:o0 + n_],
                                   reads=["a_cvo"], writes=[("mix_d", b, "conv", o0)], anchor="a8")
            kb.barrier()

        def mix_out(l, ctx_out):
            with ES() as es:
                wo = S(es, "wo", [128, KC, D], BF16)
                kb.dma("pool", wo[:], w_out[l, :, :].rearrange("(c p) n -> p c n", p=128), writes=["wo"], anchor="wo")
                A, gv = load_rowconsts(es, l, 1, 1.0)
                mx = [S(es, "mx%d" % i, [128, 8, 512], BF16) for i in range(2)]
                xin = [S(es, "xin%d" % i, [128, 4, D]) for i in range(2)]
                junk = S(es, "junk", [128, D], BF16)
                ss2 = S(es, "ss2", [128, 8]); rstd2 = S(es, "rstd2", [128, 8])
                tt = S(es, "tt", [128, D])
                pd = [PS(es, "pd%d" % i, [128, 1024]) for i in range(2)]
                groups = []
                for (b, r, p0, n) in segs(ctx_out, True):
                    for g0 in range(0, n, 512):
                        groups.append((b, r, p0 + g0, min(512, n - g0)))

                def load(gi):
                    b, r, pos, n = groups[gi]
                    sl = gi % 2
                    kb.dma("sp", xin[sl][:, 0:n // 128, :], xres[b, pos:pos + n, :].rearrange("(t p) d -> p t d", p=128),
                           writes=[("xin", sl)], anchor="xl%d" % sl)
                    kb.dma("sp", mx[sl][:, :, 0:n], mix_d[b, :, :, pos:pos + n].rearrange("c p n -> p c n"),
                           writes=[("mx", sl)], anchor="ml%d" % sl)
                load(0)
                for gi, (b, r, pos, n) in enumerate(groups):
                    sl = gi % 2
                    if gi + 1 < len(groups):
                        load(gi + 1)
                    for t in range(n // 128):
                        ps = t % 2
                        for nh in range(2):
                            for c in range(KC):
                                kb.op("pe", lambda e, c=c, nh=nh, t=t, ps=ps, sl=sl: e.matmul(
                                    pd[ps][:, nh * 512:(nh + 1) * 512], lhsT=mx[sl][:, c, t * 128:(t + 1) * 128],
                                    rhs=wo[:, c, nh * 512:(nh + 1) * 512], start=(c == 0), stop=(c == KC - 1)),
                                    reads=[("mx", sl), "wo"], writes=[("pd", ps)])
                        post_residual(pd[ps], ("pd", ps), xin[sl][:, t, :], ("xin", sl), gv[r], ("gvec", r), junk, ss2, rstd2, tt, t)
                    kb.dma("pool", xres[b, pos:pos + n, :].rearrange("(t p) d -> p t d", p=128), xin[sl][:, 0:n // 128, :],
                           reads=[("xin", sl)], writes=[("xst", b, pos)], anchor="pxs%d" % sl)
            kb.barrier()

        import os
        stop = int(os.environ.get("KSTOP", "99"))
        ph = 0
        for l in range(depth):
            last = (l == depth - 1)
            for fn in (lambda: ffn_phase(l, 0, 0, True, False), lambda: mix_proj(l, not last), lambda: ssm_phase(l, not last),
                       lambda: attn_phase(l, not last), lambda: mix_out(l, not last), lambda: ffn_phase(l, 2, 1, not last, last)):
                ph += 1
                if ph <= stop:
                    fn()
        kb.barrier()
        if stop < 6 * depth:
            for b in range(NB):
                kb.dma("sp", y_out[b, :, :], xres[b, NCTX:NP, :], writes=[("ydump", b)], anchor="cp")
            kb.barrier()
    return nc


def _consts(L):
    NT = L // 128
    c = {}
    c["ident"] = np.eye(128, dtype=np.float32)
    j = np.arange(128)[:, None]
    i = np.arange(128)[None, :]
    c["m_prev"] = (j >= i).astype(np.float32)
    c["m_next"] = (j <= i).astype(np.float32)
    bi = (np.arange(128) // 16)
    c["tmask_f"] = (bi[None, :] >= bi[:, None]).astype(np.float32)
    c["tmask_b"] = (bi[:, None] >= bi[None, :]).astype(np.float32)
    gh = np.zeros((128, 2), np.float32)
    gh[:64, 0] = 1.0
    gh[64:, 1] = 1.0
    c["gh_mask"] = gh
    pos = np.arange(L, dtype=np.float32)
    row = np.floor(pos / 64.0).astype(np.float32)
    col = (pos - row * 64.0).astype(np.float32)
    inv = (np.float32(10000.0) ** (-np.arange(16, dtype=np.float32) / np.float32(16.0))).astype(np.float32)
    ang = np.stack([row[:, None] * inv[None, :], col[:, None] * inv[None, :]], axis=1).astype(np.float32)
    tab = np.concatenate([np.cos(ang).reshape(L, 32), np.sin(ang).reshape(L, 32)], axis=1).astype(np.float32)
    c["rope"] = np.ascontiguousarray(tab.reshape(NT, 128, 64).transpose(1, 0, 2))
    return c


def _shared_inputs(inp, depth, L):
    f = lambda a: np.ascontiguousarray(np.asarray(a, dtype=np.float32))
    sh = {}
    for k in ("w_ada", "b_ada", "norm_post", "ffn_w_gate", "ffn_w_up", "ffn_w_down", "w_in", "w_out"):
        sh[k] = f(inp[k])[:depth]
    npre = f(inp["norm_pre"])[:depth]
    sh["gpreT"] = np.ascontiguousarray(npre.reshape(depth * 3, KC, 128).transpose(2, 0, 1))
    sh["sink"] = np.ascontiguousarray(np.broadcast_to(f(inp["attn_sink"])[:depth, None, :], (depth, 128, 8)))

    def dq(a):
        a = f(a)[:depth].reshape(depth, 2, 2, 8, 64)
        return np.ascontiguousarray(a.transpose(0, 2, 4, 1, 3).reshape(depth, 128, 16))
    sh["lam_re"] = dq(inp["ssm_lambda_re"])
    sh["lam_im"] = dq(inp["ssm_lambda_im"])
    ls = f(inp["ssm_log_step"])[:depth]
    sh["lstep"] = dq(np.broadcast_to(ls[..., None], ls.shape + (64,)))

    def bq(a):
        a = f(a)[:depth].reshape(depth, 2, 2, 8, 64, 16)
        return np.ascontiguousarray(a.transpose(0, 2, 4, 1, 3, 5).reshape(depth, 128, 256))

    def cq(a):
        a = f(a)[:depth].reshape(depth, 2, 2, 8, 16, 64)
        return np.ascontiguousarray(a.transpose(0, 2, 5, 1, 3, 4).reshape(depth, 128, 256))
    sh["b_re"] = bq(inp["ssm_b_re"]); sh["b_im"] = bq(inp["ssm_b_im"])
    sh["c_re"] = cq(inp["ssm_c_re"]); sh["c_im"] = cq(inp["ssm_c_im"])
    dsk = f(inp["ssm_d"])[:depth].reshape(depth, 16, 16)
    sh["d_skip"] = np.ascontiguousarray(np.broadcast_to(dsk.transpose(0, 2, 1)[:, None, :, :], (depth, 8, 16, 16)).reshape(depth, 128, 16))
    sh["w_glu"] = f(inp["ssm_w_glu"])[:depth]
    sh["b_glu"] = np.ascontiguousarray(f(inp["ssm_b_glu"])[:depth].reshape(depth, 2, 128).transpose(0, 2, 1))
    cw = f(inp["conv_w"])[:depth]
    sh["conv_w"] = np.ascontiguousarray(cw.reshape(depth, 3, 2, 128).transpose(0, 3, 2, 1).reshape(depth, 128, 6))
    sh.update(_consts(L))
    return sh


_CACHE = {}


def run(inp, depth=DEPTH, n_cores=8, trace=False):
    x = np.asarray(inp["x"], dtype=np.float32)
    L = x.shape[1]
    key = (L, depth)
    if key not in _CACHE:
        _CACHE[key] = build(L, depth)
    nc = _CACHE[key]
    sh = _shared_inputs(inp, depth, L)
    c = np.asarray(inp["c"], dtype=np.float32)
    ctx = np.asarray(inp["ctx"], dtype=np.float32)
    cc = np.asarray(inp["c_ctx"], dtype=np.float32)
    in_maps = []
    for i in range(n_cores):
        m = dict(sh)
        m["x"] = np.ascontiguousarray(x[NB * i:NB * i + NB])
        m["ctx"] = np.ascontiguousarray(ctx[NB * i:NB * i + NB])
        cv = np.stack([c[NB * i], c[NB * i + 1], cc], axis=0)
        m["cT"] = np.ascontiguousarray(cv.reshape(3, KC, 128).transpose(2, 1, 0))
        in_maps.append(m)
    res = run_bass_kernel_spmd(nc, in_maps, core_ids=list(range(n_cores)))
    return np.concatenate([np.asarray(r["y"], dtype=np.float32) for r in res.results], axis=0)


def kernel(**inputs):
    return run(inputs, DEPTH, 8)
```

```python
import math
import contextlib
import numpy as np
import ml_dtypes
import concourse.bass as bass
import concourse.mybir as mybir
from concourse.bass_utils import run_bass_kernel_spmd

F32 = mybir.dt.float32
BF16 = mybir.dt.bfloat16
I32 = mybir.dt.int32
AF = mybir.ActivationFunctionType
ALU = mybir.AluOpType

D = 1024
KC = 8
FF = 2816
FC = 22
NCTX = 256
DEPTH = 4
SEQ = 4096
NB = 2
EPS = 1e-6
TWO_PI = 2.0 * math.pi

EOBJ = {"pe": "tensor", "act": "scalar", "dve": "vector", "pool": "gpsimd", "sp": "sync"}


class KB:
    def __init__(self, nc):
        self.nc = nc
        self.sem = {}
        self.cnt = {}
        self.waited = {e: {} for e in EOBJ}
        self.res = {}
        for e in EOBJ:
            self._mk("eng_" + e)

    def _mk(self, s):
        self.sem[s] = self.nc.alloc_semaphore(s)
        self.cnt[s] = 0

    def eng(self, e):
        return getattr(self.nc, EOBJ[e])

    def _r(self, k):
        r = self.res.get(k)
        if r is None:
            r = self.res[k] = [{}, {}]
        return r

    PSUMK = {"pm", "pmt", "ptr", "pg", "pd", "pq", "pqt", "pz", "pu", "pt", "pw", "pz0", "pz1", "py", "pgt", "pgl",
             "pst", "pnum", "pden"}

    def _deps(self, reads, writes, own=None):
        d = {}
        for k in reads:
            r = self._r(k)
            for s, v in r[0].items():
                if d.get(s, 0) < v:
                    d[s] = v
            if (k if isinstance(k, str) else k[0]) in self.PSUMK:
                for s, v in r[1].items():
                    if s != own and d.get(s, 0) < v:
                        d[s] = v
        for k in writes:
            r = self._r(k)
            for m in r:
                for s, v in m.items():
                    if d.get(s, 0) < v:
                        d[s] = v
        return d

    def _waits(self, e, deps, skip=None):
        w = self.waited[e]
        for s, v in deps.items():
            if s == skip:
                continue
            if w.get(s, 0) < v:
                w[s] = v
                self.eng(e).wait_ge(self.sem[s], v)

    def _rec(self, reads, writes, s, v):
        for k in reads:
            r = self._r(k)[1]
            if r.get(s, 0) < v:
                r[s] = v
        for k in writes:
            r = self._r(k)
            r[0] = {s: v}
            r[1] = {}

    def op(self, e, fn, reads=(), writes=()):
        deps = self._deps(reads, writes, own="eng_" + e)
        self._waits(e, deps, skip=("eng_pe" if e == "pe" else None))
        ins = fn(self.eng(e))
        s = "eng_" + e
        self.cnt[s] += 1
        ins.then_inc(self.sem[s], 1)
        self._rec(reads, writes, s, self.cnt[s])

    def dma(self, q, out, in_, reads=(), writes=(), anchor=None):
        s = "dma_" + anchor
        if s not in self.sem:
            self._mk(s)
        deps = self._deps(reads, writes)
        self._waits(q, deps, skip=s)
        ins = self.eng(q).dma_start(out=out, in_=in_)
        self.cnt[s] += 16
        ins.then_inc(self.sem[s], 16)
        self._rec(reads, writes, s, self.cnt[s])

    def barrier(self):
        for e in EOBJ:
            w = self.waited[e]
            for s, v in self.cnt.items():
                if v > 0 and w.get(s, 0) < v:
                    w[s] = v
                    self.eng(e).wait_ge(self.sem[s], v)
        self.res = {}


def build(L=SEQ, depth=DEPTH, dbg=False):
    nc = bass.Bass("TRN2", target_bir_lowering=False)
    kb = KB(nc)
    NP = NCTX + L
    NT = L // 128
    NCH = NP // 8
    NCC = NCTX // 8
    NLC = L // 8
    NKC = NP // 128
    nlev = int(math.ceil(math.log2(NCH)))

    def din(name, shape, dt=F32):
        return nc.dram_tensor(name, list(shape), dt, kind="ExternalInput").ap()

    def dscr(name, shape, dt):
        return nc.dram_tensor(name, list(shape), dt, kind="Internal").ap()

    x_in = din("x", [NB, L, D])
    ctx_in = din("ctx", [NB, NCTX, D])
    cT_in = din("cT", [128, KC, 3])
    w_ada = din("w_ada", [depth, D, 9 * D])
    b_ada = din("b_ada", [depth, 9 * D])
    gpreT_in = din("gpreT", [128, depth * 3, KC])
    norm_post = din("norm_post", [depth, 3, D])
    wgate = din("ffn_w_gate", [depth, 2, D, FF])
    wup = din("ffn_w_up", [depth, 2, D, FF])
    wdown = din("ffn_w_down", [depth, 2, FF, D])
    w_in = din("w_in", [depth, D, 1792])
    w_out = din("w_out", [depth, D, D])
    sink_in = din("sink", [depth, 128, 8])
    lam_re = din("lam_re", [depth, 128, 16])
    lam_im = din("lam_im", [depth, 128, 16])
    lstep = din("lstep", [depth, 128, 16])
    bre_in = din("b_re", [depth, 128, 256])
    bim_in = din("b_im", [depth, 128, 256])
    cre_in = din("c_re", [depth, 128, 256])
    cim_in = din("c_im", [depth, 128, 256])
    dsk_in = din("d_skip", [depth, 128, 16])
    wglu_in = din("w_glu", [depth, 256, 256])
    bglu_in = din("b_glu", [depth, 128, 2])
    convw_in = din("conv_w", [depth, 128, 6])
    ident_in = din("ident", [128, 128])
    mprev_in = din("m_prev", [128, 128])
    mnext_in = din("m_next", [128, 128])
    tmf_in = din("tmask_f", [128, 128])
    tmb_in = din("tmask_b", [128, 128])
    ghm_in = din("gh_mask", [128, 2])
    rope_in = din("rope", [128, NT, 64])
    y_out = nc.dram_tensor("y", [NB, L, D], F32, kind="ExternalOutput").ap()

    xres = dscr("xres", [NB, NP, D], F32)
    modr = dscr("modr", [depth, 3, 9, D], F32)
    qT_d = dscr("qT_d", [NB, 4, 128, NP], BF16)
    kT_d = dscr("kT_d", [NB, 128, NP], BF16)
    v_d = dscr("v_d", [NB, NP, 128], BF16)
    U_d = dscr("U_d", [NB, 128, 16, NCH], BF16)
    gb_d = dscr("gb_d", [NB, 2, 128, NP], BF16)
    gz_d = dscr("gz_d", [NB, 2, 128, NP], BF16)
    mix_d = dscr("mix_d", [NB, 8, 128, NP], BF16)

    ES = contextlib.ExitStack

    uid = [0]

    def S(es, name, shape, dt=F32):
        uid[0] += 1
        return es.enter_context(nc.sbuf_tensor("sb%d_%s" % (uid[0], name), list(shape), dt))

    def PS(es, name, shape, dt=F32):
        uid[0] += 1
        return es.enter_context(nc.psum_tensor("ps%d_%s" % (uid[0], name), list(shape), dt))

    glob = ES()
    with glob:
        ident_f = S(glob, "ident_f", [128, 128])
        ident_b = S(glob, "ident_b", [128, 128], BF16)
        modT = S(glob, "modT", [128, depth * 9, KC, 3])
        gpreT = S(glob, "gpreT", [128, depth * 3, KC])
        kb.dma("sp", ident_f[:], ident_in[:, :], writes=["ident_f"], anchor="c0")
        kb.dma("sp", gpreT[:], gpreT_in[:, :, :], writes=["gpreT"], anchor="c1")
        kb.op("dve", lambda e: e.tensor_copy(ident_b[:], ident_f[:]), reads=["ident_f"], writes=["ident_b"])

        for b in range(NB):
            kb.dma("sp", xres[b, 0:NCTX, :], ctx_in[b, :, :], writes=[("xres", b, "c")], anchor="cp")
            kb.dma("sp", xres[b, NCTX:NP, :], x_in[b, :, :], writes=[("xres", b, "x")], anchor="cp")

        with ES() as es:
            cT = S(es, "cT", [128, KC, 3])
            scT = S(es, "scT", [128, KC, 3])
            wa = [S(es, "wa%d" % i, [128, KC, D]) for i in range(2)]
            bias3 = [S(es, "bias3_%d" % i, [3, D]) for i in range(2)]
            mrow = [S(es, "mrow%d" % i, [3, D]) for i in range(2)]
            pm = [PS(es, "pm%d" % i, [128, 1024]) for i in range(2)]
            pmt = [PS(es, "pmt%d" % i, [128, 512]) for i in range(2)]
            kb.dma("sp", cT[:], cT_in[:, :, :], writes=["cT"], anchor="c2")
            kb.op("act", lambda e: e.activation(out=scT[:], in_=cT[:], func=AF.Silu), reads=["cT"], writes=["scT"])
            it = 0
            for l in range(depth):
                for j in range(9):
                    sl = it % 2
                    it += 1
                    kb.dma("sp" if sl == 0 else "pool", wa[sl][:], w_ada[l, :, j * D:(j + 1) * D].rearrange("(k p) c -> p k c", p=128),
                           writes=[("wa", sl)], anchor="%swa%d" % ("" if sl == 0 else "p", sl))
                    kb.dma("sp", bias3[sl][:], b_ada[l:l + 1, j * D:(j + 1) * D].partition_broadcast(3),
                           writes=[("bias3", sl)], anchor="b3%d" % sl)
                    for nh in range(2):
                        for k in range(KC):
                            kb.op("pe", lambda e, k=k, nh=nh, sl=sl: e.matmul(
                                pm[sl][0:3, nh * 512:(nh + 1) * 512], lhsT=scT[:, k, :],
                                rhs=wa[sl][:, k, nh * 512:(nh + 1) * 512], start=(k == 0), stop=(k == KC - 1)),
                                reads=["scT", ("wa", sl)], writes=[("pm", sl)])
                    kb.op("dve", lambda e, sl=sl: e.tensor_tensor(mrow[sl][:], pm[sl][0:3, :], bias3[sl][:], ALU.add),
                          reads=[("pm", sl), ("bias3", sl)], writes=[("mrow", sl)])
                    kb.dma("sp", modr[l, :, j, :], mrow[sl][:], reads=[("mrow", sl)], writes=[("modr", l, j)],
                           anchor="mst%d" % sl)
                    for k in range(KC):
                        kb.op("pe", lambda e, k=k, sl=sl: e.transpose(
                            pmt[sl][:, k * 3:k * 3 + 3], mrow[sl][0:3, k * 128:(k + 1) * 128], ident_f[0:3, 0:3]),
                            reads=[("mrow", sl), "ident_f"], writes=[("pmt", sl)])
                    kb.op("dve", lambda e, sl=sl, l=l, j=j: e.tensor_copy(
                        modT[:, l * 9 + j, :, :], pmt[sl][:, 0:24].rearrange("p (k r) -> p k r", r=3)),
                        reads=[("pmt", sl)], writes=["modT"])
        kb.barrier()

        def segs(with_ctx=True, with_lat=True):
            out = []
            for b in range(NB):
                if with_ctx:
                    out.append((b, 2, 0, NCTX))
                if with_lat:
                    out.append((b, b, NCTX, L))
            return out

        def load_rowconsts(es, l, s, fac):
            A = S(es, "Acol", [128, 3, KC])
            gp = S(es, "gp_bc", [128, D])
            gt = S(es, "gate_bc", [128, D])
            gv = [S(es, "gvec%d" % r, [128, D]) for r in range(3)]
            for r in range(3):
                kb.op("dve", lambda e, r=r: e.scalar_tensor_tensor(
                    A[:, r, :], modT[:, l * 9 + 3 * s + 1, :, r], 1.0, gpreT[:, l * 3 + s, :], ALU.add, ALU.mult),
                    reads=["modT", "gpreT"], writes=["Acol"])
            kb.dma("sp", gp[:], norm_post[l, s:s + 1, :].partition_broadcast(128), writes=["gp_bc"], anchor="gp")
            for r in range(3):
                kb.dma("sp", gt[:], modr[l, r, 3 * s + 2:3 * s + 3, :].partition_broadcast(128),
                       reads=[("modr", l, 3 * s + 2)], writes=["gate_bc"], anchor="gt")
                kb.op("dve", lambda e, r=r: e.scalar_tensor_tensor(gv[r][:], gt[:], fac, gp[:], ALU.mult, ALU.mult),
                      reads=["gate_bc", "gp_bc"], writes=[("gvec", r)])
            return A, gv

        def norm_to_hT(xin_t, xkey, ntile, r, l, s, A, xn, junk, ss, rstd, ptr, hT, hkey):
            for t in range(ntile):
                kb.op("act", lambda e, t=t: e.activation(out=junk[:], in_=xin_t[:, t, :], func=AF.Square,
                                                        accum_out=ss[:, t:t + 1]),
                      reads=[xkey], writes=["junk", ("ss", t)])
            kb.op("act", lambda e: e.activation(out=rstd[:, 0:ntile], in_=ss[:, 0:ntile], func=AF.Sqrt,
                                                scale=1.0 / D, bias=EPS),
                  reads=[("ss", t) for t in range(ntile)], writes=["rstd"])
            kb.op("dve", lambda e: e.reciprocal(rstd[:, 0:ntile], rstd[:, 0:ntile]), reads=["rstd"], writes=["rstd"])
            for t in range(ntile):
                xs = t % 2
                kb.op("dve", lambda e, t=t, xs=xs: e.tensor_scalar(xn[xs][:], xin_t[:, t, :], rstd[:, t:t + 1], None,
                                                                  ALU.mult),
                      reads=[xkey, "rstd"], writes=[("xn", xs)])
                for k in range(KC):
                    kb.op("pe", lambda e, k=k, xs=xs: e.transpose(ptr[xs][:, k * 128:(k + 1) * 128],
                                                                  xn[xs][:, k * 128:(k + 1) * 128], ident_b[:]),
                          reads=[("xn", xs), "ident_b"], writes=[("ptr", xs)])
                for k in range(KC):
                    kb.op("dve", lambda e, k=k, t=t, xs=xs: e.tensor_scalar(
                        hT[:, k, t * 128:(t + 1) * 128], ptr[xs][:, k * 128:(k + 1) * 128],
                        A[:, r, k:k + 1], modT[:, l * 9 + 3 * s, k, r:r + 1], ALU.mult, ALU.add),
                        reads=[("ptr", xs), "Acol", "modT"], writes=[(hkey, k)])

        def norm_stats(xin_t, xkey, ntile, junk, ss, rstd, tag):
            for t in range(ntile):
                kb.op("act", lambda e, t=t: e.activation(out=junk[:], in_=xin_t[:, t, :], func=AF.Square,
                                                        accum_out=ss[:, t:t + 1]),
                      reads=[xkey], writes=["junk", ("ss", tag, t)])
            kb.op("act", lambda e: e.activation(out=rstd[:, 0:ntile], in_=ss[:, 0:ntile], func=AF.Sqrt,
                                                scale=1.0 / D, bias=EPS),
                  reads=[("ss", tag, t) for t in range(ntile)], writes=[("rstd", tag)])
            kb.op("dve", lambda e: e.reciprocal(rstd[:, 0:ntile], rstd[:, 0:ntile]), reads=[("rstd", tag)],
                  writes=[("rstd", tag)])

        def norm_xn(xin_t, xkey, ntile, xn, rstd, tag):
            for t in range(ntile):
                kb.op("dve", lambda e, t=t: e.tensor_scalar(xn[t][:], xin_t[:, t, :], rstd[:, t:t + 1], None, ALU.mult),
                      reads=[xkey, ("rstd", tag)], writes=[("xn", t)])

        def norm_tr(ntile, xn, ptr):
            for t in range(ntile):
                for k in range(KC):
                    kb.op("pe", lambda e, k=k, t=t: e.transpose(ptr[t][:, k * 128:(k + 1) * 128],
                                                                xn[t][:, k * 128:(k + 1) * 128], ident_b[:]),
                          reads=[("xn", t), "ident_b"], writes=[("ptr", t)])

        def norm_evac(ntile, r, l, s, A, ptr, hT, hkey):
            for t in range(ntile):
                for k in range(KC):
                    kb.op("dve", lambda e, k=k, t=t: e.tensor_scalar(
                        hT[:, k, t * 128:(t + 1) * 128], ptr[t][:, k * 128:(k + 1) * 128],
                        A[:, r, k:k + 1], modT[:, l * 9 + 3 * s, k, r:r + 1], ALU.mult, ALU.add),
                        reads=[("ptr", t), "Acol", "modT"], writes=[(hkey, k)])

        def post_residual(py, pykey, xin_ap, xkey, gv_r, gkey, junk, ss2, rstd2, tt, t):
            import os
            kp = int(os.environ.get("KPOST", "99"))
            if kp <= 0:
                return
            kb.op("act", lambda e: e.activation(out=junk[:], in_=py[:], func=AF.Square, accum_out=ss2[:, t:t + 1]),
                  reads=[pykey], writes=["junk", ("ss2", t)])
            if kp <= 1:
                return
            for hh in range(2):
                kv = os.environ.get("KVAR", "")
                if kv == "a":
                    kb.op("dve", lambda e, hh=hh: e.tensor_copy(tt[:, hh * 512:(hh + 1) * 512], py[:, hh * 512:(hh + 1) * 512]),
                          reads=[pykey, gkey], writes=["tt"])
                elif kv == "b":
                    kb.op("dve", lambda e, hh=hh: e.tensor_tensor(tt[:, hh * 512:(hh + 1) * 512], gv_r[:, hh * 512:(hh + 1) * 512],
                                                                  gv_r[:, hh * 512:(hh + 1) * 512], ALU.mult),
                          reads=[pykey, gkey], writes=["tt"])
                elif kv == "c":
                    kb.op("dve", lambda e, hh=hh: e.tensor_copy(tt[:, hh * 512:(hh + 1) * 512], gv_r[:, hh * 512:(hh + 1) * 512]),
                          reads=[gkey], writes=["tt"])
                else:
                    kb.op("dve", lambda e, hh=hh: e.tensor_tensor(tt[:, hh * 512:(hh + 1) * 512], gv_r[:, hh * 512:(hh + 1) * 512],
                                                                  py[:, hh * 512:(hh + 1) * 512], ALU.mult),
                          reads=[pykey, gkey, ("ss2", t)], writes=["tt"])
            if kp <= 2:
                return
            kb.op("act", lambda e: e.activation(out=rstd2[:, t:t + 1], in_=ss2[:, t:t + 1], func=AF.Sqrt,
                                                scale=1.0 / D, bias=EPS),
                  reads=[("ss2", t)], writes=[("rstd2", t)])
            kb.op("dve", lambda e: e.reciprocal(rstd2[:, t:t + 1], rstd2[:, t:t + 1]),
                  reads=[("rstd2", t)], writes=[("rstd2", t)])
            kb.op("dve", lambda e: e.scalar_tensor_tensor(xin_ap, tt[:], rstd2[:, t:t + 1], xin_ap, ALU.mult, ALU.add),
                  reads=["tt", ("rstd2", t), xkey], writes=[xkey])

        def ffn_phase(l, s, fi, with_ctx, final):
            G = 256
            with ES() as es:
                wg = S(es, "wg", [128, KC, FF], BF16)
                wu = S(es, "wu", [128, KC, FF], BF16)
                wd = S(es, "wd", [128, FC, D], BF16)
                for k in range(KC):
                    kb.dma("pool", wg[:, k, :], wgate[l, fi, k * 128:(k + 1) * 128, :], writes=["wg"], anchor="wg")
                    kb.dma("pool", wu[:, k, :], wup[l, fi, k * 128:(k + 1) * 128, :], writes=["wu"], anchor="wu")
                for f in range(0, FC, 2):
                    kb.dma("pool", wd[:, f:f + 2, :],
                           wdown[l, fi, f * 128:(f + 2) * 128, :].rearrange("(f p) d -> p f d", p=128),
                           writes=["wd"], anchor="wd")
                import os
                sub = int(os.environ.get("KSUB", "99"))
                if sub <= 1:
                    kb.barrier(); return
                A, gv = load_rowconsts(es, l, s, 0.5)
                if sub <= 2:
                    kb.barrier(); return
                xin = [S(es, "xin%d" % i, [128, 2, D]) for i in range(2)]
                xn = [S(es, "xn%d" % i, [128, D], BF16) for i in range(2)]
                junk = S(es, "junk", [128, D], BF16)
                ssl = [S(es, "ss%d" % i, [128, 8]) for i in range(2)]
                rstdl = [S(es, "rstd%d" % i, [128, 8]) for i in range(2)]
                ss2 = S(es, "ss2", [128, 8]); rstd2 = S(es, "rstd2", [128, 8])
                hTl = [S(es, "hT%d" % i, [128, KC, G], BF16) for i in range(2)]
                actT = S(es, "actT", [128, FC, G], BF16)
                sg = [S(es, "sg%d" % i, [128, G]) for i in range(2)]
                tt = S(es, "tt", [128, D])
                ptr = [PS(es, "ptr%d" % i, [128, 1024], BF16) for i in range(2)]
                pg = [PS(es, "pg%d" % i, [128, 512]) for i in range(2)]
                pd = [PS(es, "pd%d" % i, [128, 1024]) for i in range(2)]
                groups = []
                for (b, r, p0, n) in segs(with_ctx, True):
                    for g0 in range(0, n, G):
                        groups.append((b, r, p0 + g0))
                groups = groups[:int(os.environ.get("KGRP", "9999"))]

                def load(gi):
                    b, r, pos = groups[gi]
                    sl = gi % 2
                    kb.dma("sp", xin[sl][:], xres[b, pos:pos + G, :].rearrange("(t p) d -> p t d", p=128),
                           writes=[("xin", sl)], anchor="xl%d" % sl)

                def nstats(gi):
                    sl = gi % 2
                    norm_stats(xin[sl], ("xin", sl), 2, junk, ssl[sl], rstdl[sl], sl)

                def nxn(gi):
                    sl = gi % 2
                    norm_xn(xin[sl], ("xin", sl), 2, xn, rstdl[sl], sl)

                def nevac(gi):
                    sl = gi % 2
                    norm_evac(2, groups[gi][1], l, s, A, ptr, hTl[sl], ("hT", sl))

                load(0)
                if len(groups) > 1:
                    load(1)
                nstats(0); nxn(0); norm_tr(2, xn, ptr); nevac(0)
                for gi, (b, r, pos) in enumerate(groups):
                    sl = gi % 2
                    hT = hTl[sl]
                    nxt = gi + 1 < len(groups)
                    for f in range(FC):
                        ps = f % 2
                        for half, w in ((0, wg), (1, wu)):
                            for k in range(KC):
                                kb.op("pe", lambda e, k=k, f=f, ps=ps, half=half, w=w: e.matmul(
                                    pg[ps][:, half * G:(half + 1) * G], lhsT=w[:, k, f * 128:(f + 1) * 128],
                                    rhs=hT[:, k, :], start=(k == 0), stop=(k == KC - 1)),
                                    reads=[(("hT", sl), k), "wg" if half == 0 else "wu"], writes=[("pg", ps)])
                        kb.op("act", lambda e, ps=ps: e.activation(out=sg[ps][:], in_=pg[ps][:, 0:G], func=AF.Silu),
                              reads=[("pg", ps)], writes=[("sg", ps)])
                        kb.op("dve", lambda e, ps=ps, f=f: e.tensor_tensor(actT[:, f, :], sg[ps][:], pg[ps][:, G:2 * G],
                                                                        ALU.mult),
                              reads=[("sg", ps), ("pg", ps)], writes=[("actT", f)])
                        if nxt and f == 6:
                            nstats(gi + 1)
                        if nxt and f == 12:
                            nxn(gi + 1)
                    if nxt:
                        norm_tr(2, xn, ptr)
                    for t in range(2):
                        for nh in range(2):
                            for f in range(FC):
                                kb.op("pe", lambda e, t=t, nh=nh, f=f: e.matmul(
                                    pd[t][:, nh * 512:(nh + 1) * 512], lhsT=actT[:, f, t * 128:(t + 1) * 128],
                                    rhs=wd[:, f, nh * 512:(nh + 1) * 512], start=(f == 0), stop=(f == FC - 1)),
                                    reads=[("actT", f), "wd"], writes=[("pd", t)])
                        if nxt and t == 0:
                            nevac(gi + 1)
                        post_residual(pd[t], ("pd", t), xin[sl][:, t, :], ("xin", sl), gv[r], ("gvec", r), junk, ss2, rstd2, tt, t)
                    if final and pos >= NCTX:
                        dst = y_out[b, pos - NCTX:pos - NCTX + G, :]
                    else:
                        dst = xres[b, pos:pos + G, :]
                    kb.dma("sp", dst.rearrange("(t p) d -> p t d", p=128), xin[sl][:],
                           reads=[("xin", sl)], writes=[("xst", b, pos)], anchor="xs%d" % sl)
                    if gi + 2 < len(groups):
                        load(gi + 2)
            kb.barrier()

        def mix_proj(l, ctx_out):
            with ES() as es:
                wi = S(es, "wi", [128, KC, 1792], BF16)
                for k in range(KC):
                    kb.dma("pool", wi[:, k, :], w_in[l, k * 128:(k + 1) * 128, :], writes=["wi"], anchor="wi")
                rope = S(es, "rope", [128, NT, 64])
                kb.dma("sp", rope[:], rope_in[:, :, :], writes=["rope"], anchor="rope")
                A = S(es, "Acol", [128, 3, KC])
                for r in range(3):
                    kb.op("dve", lambda e, r=r: e.scalar_tensor_tensor(
                        A[:, r, :], modT[:, l * 9 + 4, :, r], 1.0, gpreT[:, l * 3 + 1, :], ALU.add, ALU.mult),
                        reads=["modT", "gpreT"], writes=["Acol"])
                xinl = [S(es, "xin%d" % i, [128, 8, D]) for i in range(2)]
                xn = [S(es, "xn%d" % i, [128, D], BF16) for i in range(8)]
                ssl = [S(es, "ssl%d" % i, [128, 8]) for i in range(2)]
                rstdl = [S(es, "rstdl%d" % i, [128, 8]) for i in range(2)]
                junk = S(es, "junk", [128, D], BF16)
                ss = S(es, "ss", [128, 8]); rstd = S(es, "rstd", [128, 8])
                hTl = [S(es, "hT%d" % i, [128, KC, 1024], BF16) for i in range(2)]
                qk_tm = S(es, "qk_tm", [128, 640], BF16)
                v_tm = S(es, "v_tm", [128, 128], BF16)
                ra = S(es, "ra", [128, 320]); rb = S(es, "rb", [128, 320])
                rc = S(es, "rc", [128, 320]); rd = S(es, "rd", [128, 320])
                qkT = S(es, "qkT", [128, 5, 128], BF16)
                gb_sb = S(es, "gb_sb", [128, 2, 512], BF16)
                gc_sb = S(es, "gc_sb", [128, 2, 512], BF16)
                gz_sb = S(es, "gz_sb", [128, 2, 512], BF16)
                UT = S(es, "UT", [128, 16, 8, 16], BF16)
                Usb = S(es, "Usb", [128, 16, 128], BF16)
                ptr = [PS(es, "ptr%d" % i, [128, 1024], BF16) for i in range(2)]
                pq = PS(es, "pq", [128, 1024])
                pqt = PS(es, "pqt", [128, 1024], BF16)
                pz = PS(es, "pz", [128, 512])
                pu = PS(es, "pu", [128, 1024])
                pUT = pu[:, :].bitcast(BF16)
                stl = []
                for (b, r, p0, n) in segs(True, True):
                    for st0 in range(0, n, 1024):
                        stl.append((b, r, p0 + st0, min(8, (n - st0) // 128)))

                def st_load(si):
                    b_, r_, pos_, nt_ = stl[si]
                    kb.dma("pool", xinl[si % 2][:, 0:nt_, :],
                           xres[b_, pos_:pos_ + nt_ * 128, :].rearrange("(t p) d -> p t d", p=128),
                           writes=[("xin", si % 2)], anchor="pxl%d" % (si % 2))

                def st_stats(si):
                    b_, r_, pos_, nt_ = stl[si]
                    norm_stats(xinl[si % 2], ("xin", si % 2), nt_, junk, ssl[si % 2], rstdl[si % 2], si % 2)

                def st_xn(si, t):
                    b_, r_, pos_, nt_ = stl[si]
                    if t < nt_:
                        kb.op("dve", lambda e: e.tensor_scalar(xn[t][:], xinl[si % 2][:, t, :], rstdl[si % 2][:, t:t + 1], None, ALU.mult),
                              reads=[("xin", si % 2), ("rstd", si % 2)], writes=[("xn", t)])

                def st_tr(si):
                    b_, r_, pos_, nt_ = stl[si]
                    for t in range(nt_):
                        for k in range(KC):
                            kb.op("pe", lambda e, k=k, t=t: e.transpose(ptr[t % 2][:, k * 128:(k + 1) * 128],
                                                                        xn[t][:, k * 128:(k + 1) * 128], ident_b[:]),
                                  reads=[("xn", t), "ident_b"], writes=[("ptr", t % 2)])
                        for k in range(KC):
                            kb.op("dve", lambda e, k=k, t=t: e.tensor_scalar(
                                hTl[si % 2][:, k, t * 128:(t + 1) * 128], ptr[t % 2][:, k * 128:(k + 1) * 128],
                                A[:, r_, k:k + 1], modT[:, l * 9 + 3, k, r_:r_ + 1], ALU.mult, ALU.add),
                                reads=[("ptr", t % 2), "Acol", "modT"], writes=[(("hT", si % 2), k)])

                def st_norm(si):
                    st_stats(si)
                    for t in range(8):
                        st_xn(si, t)
                    st_tr(si)

                st_load(0)
                if len(stl) > 1:
                    st_load(1)
                st_norm(0)
                for si, (b, r, pos, ntile) in enumerate(stl):
                    if True:
                        is_ctx = (r == 2)
                        ntok = ntile * 128
                        hT = hTl[si % 2]
                        hk = ("hT", si % 2)
                        gq = []
                        for h0 in range(0, ntok, 512):
                            nn = min(512, ntok - h0)
                            for cc in range(6):
                                gq.append((h0, nn, cc))
                        per_tile = -(-len(gq) // ntile)

                        def emit_gbz():
                            h0, nn, cc = gq.pop(0)
                            for k in range(KC):
                                kb.op("pe", lambda e, k=k: e.matmul(
                                    pz[:, 0:nn], lhsT=wi[:, k, 1024 + cc * 128:1024 + (cc + 1) * 128],
                                    rhs=hT[:, k, h0:h0 + nn], start=(k == 0), stop=(k == KC - 1)),
                                    reads=[(hk, k), "wi"], writes=["pz"])
                            if cc < 2:
                                kb.op("act", lambda e: e.activation(out=gb_sb[:, cc, 0:nn], in_=pz[:, 0:nn], func=AF.Copy),
                                      reads=["pz"], writes=["gb_sb"])
                            elif cc < 4:
                                kb.op("act", lambda e: e.activation(out=gc_sb[:, cc - 2, 0:nn], in_=pz[:, 0:nn], func=AF.Copy),
                                      reads=["pz"], writes=["gc_sb"])
                            else:
                                kb.op("dve", lambda e: e.tensor_tensor(gz_sb[:, cc - 4, 0:nn], gc_sb[:, cc - 4, 0:nn],
                                                                       pz[:, 0:nn], ALU.mult),
                                      reads=["pz", "gc_sb"], writes=["gz_sb"])
                            if cc == 5:
                                kb.dma("sp", gb_d[b, :, :, pos + h0:pos + h0 + nn].rearrange("c p n -> p c n"), gb_sb[:, :, 0:nn],
                                       reads=["gb_sb"], writes=[("gb_d", b, pos + h0)], anchor="gbst")
                                kb.dma("sp", gz_d[b, :, :, pos + h0:pos + h0 + nn].rearrange("c p n -> p c n"), gz_sb[:, :, 0:nn],
                                       reads=["gz_sb"], writes=[("gz_d", b, pos + h0)], anchor="gzst")

                        for t in range(ntile):
                            tp = pos + t * 128
                            for (c0, c1) in ((0, 512), (512, 768)):
                                for k in range(KC):
                                    kb.op("pe", lambda e, k=k, t=t, c0=c0, c1=c1: e.matmul(
                                        pq[:, c0:c1], lhsT=hT[:, k, t * 128:(t + 1) * 128], rhs=wi[:, k, c0:c1],
                                        start=(k == 0), stop=(k == KC - 1)),
                                        reads=[(hk, k), "wi"], writes=["pq"])
                            kb.op("act", lambda e: e.activation(out=v_tm[:], in_=pq[:, 640:768], func=AF.Copy),
                                  reads=["pq"], writes=["v_tm"])
                            kb.dma("sp", v_d[b, tp:tp + 128, :], v_tm[:], reads=["v_tm"], writes=[("v_d", b, tp)],
                                   anchor="vst")
                            for _ in range(per_tile):
                                if gq:
                                    emit_gbz()
                            if si + 1 < len(stl):
                                if t == 0:
                                    st_stats(si + 1)
                                if ntile == 8:
                                    st_xn(si + 1, t)
                                elif t == ntile - 1:
                                    for t2 in range(8):
                                        st_xn(si + 1, t2)
                            if is_ctx:
                                kb.op("act", lambda e: e.activation(out=qk_tm[:], in_=pq[:, 0:640], func=AF.Copy),
                                      reads=["pq"], writes=["qk_tm"])
                            else:
                                lt = (tp - NCTX) // 128
                                qv = pq[:, 0:640].rearrange("p (h a s i) -> p h a s i", h=10, a=2, s=2)
                                ov = qk_tm[:, :].rearrange("p (h a s i) -> p h a s i", h=10, a=2, s=2)
                                cosv = rope[:, lt, 0:32].rearrange("p (a i) -> p a i", a=2).unsqueeze(1).to_broadcast([128, 10, 2, 16])
                                sinv = rope[:, lt, 32:64].rearrange("p (a i) -> p a i", a=2).unsqueeze(1).to_broadcast([128, 10, 2, 16])
                                t1 = qv[:, :, :, 0, :]
                                t2 = qv[:, :, :, 1, :]
                                v4 = lambda tl: tl[:, :].rearrange("p (h a i) -> p h a i", h=10, a=2)
                                kb.op("dve", lambda e: e.tensor_tensor(v4(ra), t1, cosv, ALU.mult), reads=["pq", "rope"], writes=["ra"])
                                kb.op("dve", lambda e: e.tensor_tensor(v4(rb), t2, sinv, ALU.mult), reads=["pq", "rope"], writes=["rb"])
                                kb.op("dve", lambda e: e.tensor_tensor(v4(rc), t2, cosv, ALU.mult), reads=["pq", "rope"], writes=["rc"])
                                kb.op("dve", lambda e: e.tensor_tensor(v4(rd), t1, sinv, ALU.mult), reads=["pq", "rope"], writes=["rd"])
                                kb.op("dve", lambda e: e.tensor_tensor(ov[:, :, :, 0, :], v4(ra), v4(rb), ALU.subtract),
                                      reads=["ra", "rb"], writes=["qk_tm"])
                                kb.op("dve", lambda e: e.tensor_tensor(ov[:, :, :, 1, :], v4(rc), v4(rd), ALU.add),
                                      reads=["rc", "rd"], writes=["qk_tm"])
                            for c in range(5):
                                kb.op("pe", lambda e, c=c: e.transpose(pqt[:, c * 128:(c + 1) * 128],
                                                                       qk_tm[:, c * 128:(c + 1) * 128], ident_b[:]),
                                      reads=["qk_tm", "ident_b"], writes=["pqt"])
                            kb.op("act", lambda e: e.activation(out=qkT[:], in_=pqt[:, 0:640].rearrange("p (c n) -> p c n", c=5),
                                                                func=AF.Copy), reads=["pqt"], writes=["qkT"])
                            kb.dma("sp", qT_d[b, :, :, tp:tp + 128].rearrange("q p n -> p q n"), qkT[:, 0:4, :],
                                   reads=["qkT"], writes=[("qT_d", b, tp)], anchor="qst")
                            kb.dma("sp", kT_d[b, :, tp:tp + 128], qkT[:, 4, :], reads=["qkT"], writes=[("kT_d", b, tp)],
                                   anchor="qst")
                        while gq:
                            emit_gbz()
                        if si + 1 < len(stl):
                            st_tr(si + 1)
                        if si + 2 < len(stl):
                            st_load(si + 2)
                        nch = ntok // 8
                        for jh in range(2):
                            for jj in range(4):
                                j = jh * 4 + jj
                                for k in range(KC):
                                    kb.op("pe", lambda e, k=k, j=j, jj=jj: e.matmul(
                                        pu[0:nch, jj * 256:(jj + 1) * 256], lhsT=hT[:, k, j:ntok:8], rhs=wi[:, k, 768:1024],
                                        start=(k == 0), stop=(k == KC - 1)), reads=[(hk, k), "wi"], writes=["pu"])
                            kb.op("act", lambda e, jh=jh: e.activation(
                                out=UT[0:nch, :, jh * 4:(jh + 1) * 4, :].rearrange("p g j h -> p j g h"),
                                in_=pu[0:nch, :].rearrange("p (j g h) -> p j g h", j=4, g=16), func=AF.Copy),
                                reads=["pu"], writes=["UT"])
                        for g in range(16):
                            kb.op("pe", lambda e, g=g: e.transpose(pUT[:, g * 128:g * 128 + nch],
                                                                   UT[0:nch, g, :, :].rearrange("p j h -> p (j h)"),
                                                                   ident_b[0:nch, 0:nch]),
                                  reads=["UT", "ident_b"], writes=["pu"])
                        for gh in range(2):
                            kb.op("dve" if gh == 0 else "act", lambda e, gh=gh: (
                                e.tensor_copy(Usb[:, gh * 8:(gh + 1) * 8, 0:nch],
                                              pUT[:, gh * 1024:(gh + 1) * 1024].rearrange("p (g c) -> p g c", g=8)[:, :, 0:nch])
                                if gh == 0 else
                                e.activation(out=Usb[:, gh * 8:(gh + 1) * 8, 0:nch],
                                             in_=pUT[:, gh * 1024:(gh + 1) * 1024].rearrange("p (g c) -> p g c", g=8)[:, :, 0:nch],
                                             func=AF.Copy)),
                                reads=["pu"], writes=["Usb"])
                        c0 = pos // 8
                        kb.dma("sp", U_d[b, :, :, c0:c0 + nch], Usb[:, :, 0:nch], reads=["Usb"],
                               writes=[("U_d", b, c0)], anchor="ust")
            kb.barrier()

        def ssm_phase(l, ctx_out):
            with ES() as es:
                sm = lambda name, n: S(es, name, [128, n])
                lr = sm("s_lr", 16); li = sm("s_li", 16); ls = sm("s_ls", 16)
                kb.dma("sp", lr[:], lam_re[l, :, :], writes=["s_lr"], anchor="s0")
                kb.dma("sp", li[:], lam_im[l, :, :], writes=["s_li"], anchor="s1")
                kb.dma("sp", ls[:], lstep[l, :, :], writes=["s_ls"], anchor="s2")
                Br = sm("s_Br", 256); Bi = sm("s_Bi", 256); Cr = sm("s_Cr", 256); Ci = sm("s_Ci", 256)
                kb.dma("sp", Br[:], bre_in[l, :, :], writes=["s_Br"], anchor="s3")
                kb.dma("sp", Bi[:], bim_in[l, :, :], writes=["s_Bi"], anchor="s4")
                kb.dma("sp", Cr[:], cre_in[l, :, :], writes=["s_Cr"], anchor="s5")
                kb.dma("sp", Ci[:], cim_in[l, :, :], writes=["s_Ci"], anchor="s6")
                dsk = sm("s_dsk", 16); tmf = sm("s_tmf", 128); tmb = sm("s_tmb", 128); ghm = sm("s_ghm", 2)
                kb.dma("sp", dsk[:], dsk_in[l, :, :], writes=["s_dsk"], anchor="s7")
                kb.dma("sp", tmf[:], tmf_in[:, :], writes=["s_tmf"], anchor="s8")
                kb.dma("sp", tmb[:], tmb_in[:, :], writes=["s_tmb"], anchor="s9")
                kb.dma("sp", ghm[:], ghm_in[:, :], writes=["s_ghm"], anchor="s10")
                bglu = sm("s_bglu", 2)
                kb.dma("sp", bglu[:], bglu_in[l, :, :], writes=["s_bglu"], anchor="s11")
                wgl = S(es, "s_wgl", [128, 2, 256], BF16)
                kb.dma("pool", wgl[:], wglu_in[l, :, :].rearrange("(c p) n -> p c n", p=128), writes=["s_wgl"], anchor="s12")

                cnt = [0]

                def T(n):
                    cnt[0] += 1
                    return sm("s_t%d" % cnt[0], n)

                def dv(fn, reads, writes):
                    kb.op("dve", fn, reads=reads, writes=writes)

                def tt(o, a, b_, op, ks):
                    dv(lambda e: e.tensor_tensor(o, a, b_, op), ks[1:], ks[:1])

                dt_ = T(16); mag = T(16); ang = T(16); tf = T(16); ti = S(es, "s_ti", [128, 16], I32); tk = T(16)
                sn = T(16); cs = T(16)
                kb.op("act", lambda e: e.activation(out=dt_[:], in_=ls[:], func=AF.Exp), reads=["s_ls"], writes=["dt"])
                tt(mag[:], lr[:], dt_[:], ALU.mult, ["mag", "s_lr", "dt"])
                kb.op("act", lambda e: e.activation(out=mag[:], in_=mag[:], func=AF.Exp), reads=["mag"], writes=["mag"])
                tt(ang[:], li[:], dt_[:], ALU.mult, ["ang", "s_li", "dt"])
                for (dst, off) in ((sn, 8.0), (cs, 8.25)):
                    dv(lambda e, off=off: e.tensor_scalar(tf[:], ang[:], 1.0 / TWO_PI, off, ALU.mult, ALU.add), ["ang", "tf"], ["tf"])
                    dv(lambda e: e.tensor_copy(ti[:], tf[:]), ["tf"], ["ti"])
                    dv(lambda e: e.tensor_copy(tk[:], ti[:]), ["ti"], ["tk"])
                    tt(tf[:], tf[:], tk[:], ALU.subtract, ["tf", "tf", "tk"])
                    kb.op("act", lambda e, dst=dst: e.activation(out=dst[:], in_=tf[:], func=AF.Sin, scale=TWO_PI),
                          reads=["tf"], writes=["sc"])
                Er = S(es, "s_Er", [128, 9, 16]); Ei = S(es, "s_Ei", [128, 9, 16])
                dv(lambda e: e.memset(Er[:, 0, :], 1.0), [], ["E"])
                dv(lambda e: e.memset(Ei[:, 0, :], 0.0), ["E"], ["E"])
                tt(Er[:, 1, :], mag[:], cs[:], ALU.mult, ["E", "mag", "sc"])
                tt(Ei[:, 1, :], mag[:], sn[:], ALU.mult, ["E", "mag", "sc"])
                t1 = T(16); t2 = T(16)

                def cmul(orr, oi, ar_, ai_, br_, bi_, keyo, keys):
                    tt(t1[:], ar_, br_, ALU.mult, ["t1"] + keys)
                    tt(t2[:], ai_, bi_, ALU.mult, ["t2"] + keys)
                    tt(orr, t1[:], t2[:], ALU.subtract, [keyo, "t1", "t2"])
                    tt(t1[:], ar_, bi_, ALU.mult, ["t1"] + keys)
                    tt(t2[:], ai_, br_, ALU.mult, ["t2"] + keys)
                    tt(oi, t1[:], t2[:], ALU.add, [keyo, "t1", "t2"])

                for m in range(2, 9):
                    cmul(Er[:, m, :], Ei[:, m, :], Er[:, m - 1, :], Ei[:, m - 1, :], Er[:, 1, :], Ei[:, 1, :], "E", ["E"])
                Pr = S(es, "s_Pr", [128, nlev, 16]); Pi = S(es, "s_Pi", [128, nlev, 16]); Pn = S(es, "s_Pn", [128, nlev, 16])
                dv(lambda e: e.tensor_copy(Pr[:, 0, :], Er[:, 8, :]), ["E"], ["P"])
                dv(lambda e: e.tensor_copy(Pi[:, 0, :], Ei[:, 8, :]), ["E", "P"], ["P"])
                for k in range(1, nlev):
                    cmul(Pr[:, k, :], Pi[:, k, :], Pr[:, k - 1, :], Pi[:, k - 1, :], Pr[:, k - 1, :], Pi[:, k - 1, :], "P", ["P"])
                dv(lambda e: e.tensor_scalar(Pn[:], Pi[:], -1.0, None, ALU.mult), ["P"], ["Pn"])
                i8r = T(16); i8i = T(16); dn = T(16)
                tt(t1[:], Er[:, 8, :], Er[:, 8, :], ALU.mult, ["t1", "E"])
                tt(t2[:], Ei[:, 8, :], Ei[:, 8, :], ALU.mult, ["t2", "E"])
                tt(dn[:], t1[:], t2[:], ALU.add, ["dn", "t1", "t2"])
                dv(lambda e: e.reciprocal(dn[:], dn[:]), ["dn"], ["dn"])
                tt(i8r[:], Er[:, 8, :], dn[:], ALU.mult, ["i8", "E", "dn"])
                dv(lambda e: e.scalar_tensor_tensor(i8i[:], Ei[:, 8, :], -1.0, dn[:], ALU.mult, ALU.mult), ["E", "dn", "i8"], ["i8"])
                gr = T(16); gi = T(16); am1 = T(16); t3 = T(16)
                dv(lambda e: e.tensor_scalar(am1[:], Er[:, 1, :], -1.0, None, ALU.add), ["E"], ["am1"])
                tt(t1[:], lr[:], lr[:], ALU.mult, ["t1", "s_lr"])
                tt(t2[:], li[:], li[:], ALU.mult, ["t2", "s_li"])
                tt(dn[:], t1[:], t2[:], ALU.add, ["dn", "t1", "t2", "i8"])
                dv(lambda e: e.reciprocal(dn[:], dn[:]), ["dn"], ["dn"])
                tt(t1[:], am1[:], lr[:], ALU.mult, ["t1", "am1", "s_lr"])
                tt(t2[:], Ei[:, 1, :], li[:], ALU.mult, ["t2", "E", "s_li"])
                tt(t3[:], t1[:], t2[:], ALU.add, ["t3", "t1", "t2"])
                tt(gr[:], t3[:], dn[:], ALU.mult, ["g", "t3", "dn"])
                tt(t1[:], Ei[:, 1, :], lr[:], ALU.mult, ["t1", "E", "s_lr"])
                tt(t2[:], am1[:], li[:], ALU.mult, ["t2", "am1", "s_li"])
                tt(t3[:], t1[:], t2[:], ALU.subtract, ["t3", "t1", "t2"])
                tt(gi[:], t3[:], dn[:], ALU.mult, ["g", "t3", "dn", "g"])
                bbr = T(256); bbi = T(256); u1 = T(256); u2 = T(256)
                v3 = lambda tl: tl[:, :].rearrange("p (a h) -> p a h", h=16)
                bc16 = lambda ap: ap.unsqueeze(2).to_broadcast([128, 16, 16])
                tt(v3(u1), v3(Br), bc16(gr[:]), ALU.mult, ["u1", "s_Br", "g"])
                tt(v3(u2), v3(Bi), bc16(gi[:]), ALU.mult, ["u2", "s_Bi", "g"])
                tt(bbr[:], u1[:], u2[:], ALU.subtract, ["bb", "u1", "u2"])
                tt(v3(u1), v3(Bi), bc16(gr[:]), ALU.mult, ["u1", "s_Bi", "g"])
                tt(v3(u2), v3(Br), bc16(gi[:]), ALU.mult, ["u2", "s_Br", "g"])
                tt(bbi[:], u1[:], u2[:], ALU.add, ["bb", "u1", "u2", "bb"])
                Toep = S(es, "s_Toep", [128, 2, 16, 128], BF16)
                WcR = S(es, "s_WcR", [128, 2, 16, 128], BF16)
                WcI = S(es, "s_WcI", [128, 2, 16, 128], BF16)
                Wb = S(es, "s_Wb", [128, 2, 2, 8, 128], BF16)
                est = ES()
                XTr = S(est, "s_XTr", [128, 2, 8, 128]); XTi = S(est, "s_XTi", [128, 2, 8, 128])
                Yr = S(est, "s_Yr", [128, 2, 8, 128]); Yi = S(est, "s_Yi", [128, 2, 8, 128])
                w1 = S(est, "s_w1", [128, 8, 16]); w2 = S(est, "s_w2", [128, 8, 16])
                v3d = lambda tl, d: tl[:, d * 128:(d + 1) * 128].rearrange("p (q h) -> p q h", h=16)
                bc8 = lambda ap: ap.unsqueeze(2).to_broadcast([128, 8, 16])
                for d in range(2):
                    for i in range(8):
                        mx = (7 - i) if d == 0 else i
                        my = (i + 1) if d == 0 else (8 - i)
                        er_x = bc8(Er[:, mx, d * 8:(d + 1) * 8]); ei_x = bc8(Ei[:, mx, d * 8:(d + 1) * 8])
                        er_y = bc8(Er[:, my, d * 8:(d + 1) * 8]); ei_y = bc8(Ei[:, my, d * 8:(d + 1) * 8])
                        ox_r = XTr[:, d, :, i * 16:(i + 1) * 16]; ox_i = XTi[:, d, :, i * 16:(i + 1) * 16]
                        oy_r = Yr[:, d, :, i * 16:(i + 1) * 16]; oy_i = Yi[:, d, :, i * 16:(i + 1) * 16]
                        for (o_r, o_i, a_r, a_i, e_r, e_i, ko, ka) in (
                                (ox_r, ox_i, v3d(bbr, d), v3d(bbi, d), er_x, ei_x, "XT", "bb"),
                                (oy_r, oy_i, v3d(Cr, d), v3d(Ci, d), er_y, ei_y, "Y", "s_Cr")):
                            kin = [ka, "E", "s_Ci"]
                            tt(w1[:], a_r, e_r, ALU.mult, ["w1"] + kin)
                            tt(w2[:], a_i, e_i, ALU.mult, ["w2"] + kin)
                            tt(o_r, w1[:], w2[:], ALU.subtract, [ko, "w1", "w2"])
                            tt(w1[:], a_r, e_i, ALU.mult, ["w1"] + kin)
                            tt(w2[:], a_i, e_r, ALU.mult, ["w2"] + kin)
                            tt(o_i, w1[:], w2[:], ALU.add, [ko, "w1", "w2"])
                Ypr = S(est, "s_Ypr", [128, 2, 8, 128]); Ypi = S(est, "s_Ypi", [128, 2, 8, 128])
                z1 = S(est, "s_z1", [128, 16, 128]); z2 = S(est, "s_z2", [128, 16, 128])
                f16 = lambda tl: tl[:, :, :, :].rearrange("p d q n -> p (d q) n")
                bcn = lambda ap: ap.unsqueeze(2).to_broadcast([128, 16, 128])
                tt(z1[:], f16(Yr), bcn(i8r[:]), ALU.mult, ["z1", "Y", "i8"])
                tt(z2[:], f16(Yi), bcn(i8i[:]), ALU.mult, ["z2", "Y", "i8"])
                tt(f16(Ypr), z1[:], z2[:], ALU.subtract, ["Yp", "z1", "z2"])
                tt(z1[:], f16(Yr), bcn(i8i[:]), ALU.mult, ["z1", "Y", "i8"])
                tt(z2[:], f16(Yi), bcn(i8r[:]), ALU.mult, ["z2", "Y", "i8"])
                dv(lambda e: e.scalar_tensor_tensor(f16(Ypi), z1[:], -1.0, z2[:], ALU.mult, ALU.subtract), ["z1", "z2", "Yp"], ["Yp"])
                yzr = [S(est, "s_yzr%d" % i, [128, 128]) for i in range(2)]
                yzi = [S(est, "s_yzi%d" % i, [128, 128]) for i in range(2)]
                tsb = [S(est, "s_tsb%d" % i, [128, 128]) for i in range(2)]
                with ES() as es2:
                    pt = [PS(es2, "s_pt%d" % i, [128, 512]) for i in range(2)]
                    pw = [PS(es2, "s_pw%d" % i, [128, 512]) for i in range(2)]
                    n = 0
                    for d in range(2):
                        for g in range(16):
                            gh, q = g // 8, g % 8
                            sl = n % 2
                            n += 1
                            dv(lambda e, sl=sl, d=d, q=q, gh=gh: e.tensor_scalar(yzr[sl][:], Ypr[:, d, q, :], ghm[:, gh:gh + 1], None, ALU.mult),
                               ["Yp", "s_ghm", ("yzr", sl)], [("yzr", sl)])
                            dv(lambda e, sl=sl, d=d, q=q, gh=gh: e.tensor_scalar(yzi[sl][:], Ypi[:, d, q, :], ghm[:, gh:gh + 1], None, ALU.mult),
                               ["Yp", "s_ghm", ("yzi", sl)], [("yzi", sl)])
                            kb.op("pe", lambda e, sl=sl, d=d, q=q: e.matmul(pt[sl][:, 0:128], lhsT=XTr[:, d, q, :], rhs=yzr[sl][:],
                                                                       start=True, stop=False), reads=["XT", ("yzr", sl)], writes=[("pt", sl)])
                            kb.op("pe", lambda e, sl=sl, d=d, q=q: e.matmul(pt[sl][:, 0:128], lhsT=XTi[:, d, q, :], rhs=yzi[sl][:],
                                                                       start=False, stop=True), reads=["XT", ("yzi", sl)], writes=[("pt", sl)])
                            msk = tmf if d == 0 else tmb
                            if d == 0:
                                dv(lambda e, sl=sl, msk=msk: e.tensor_tensor(tsb[sl][:], pt[sl][:, 0:128], msk[:], ALU.mult),
                                   [("pt", sl), "s_tmf", ("tsb", sl)], [("tsb", sl)])
                                dv(lambda e, sl=sl, g=g, d=d: e.scalar_tensor_tensor(Toep[:, d, g, :], ident_f[:], dsk[:, g:g + 1], tsb[sl][:],
                                                                                 ALU.mult, ALU.add),
                                   [("tsb", sl), "ident_f", "s_dsk"], ["Toep"])
                            else:
                                dv(lambda e, sl=sl, msk=msk, g=g, d=d: e.tensor_tensor(Toep[:, d, g, :], pt[sl][:, 0:128], msk[:], ALU.mult),
                                   [("pt", sl), "s_tmb"], ["Toep"])
                            dv(lambda e, d=d, q=q, g=g, gh=gh: e.tensor_scalar(WcR[:, d, g, :], Yr[:, d, q, :], ghm[:, gh:gh + 1], None, ALU.mult),
                               ["Y", "s_ghm"], ["WcR"])
                            dv(lambda e, d=d, q=q, g=g, gh=gh: e.tensor_scalar(WcI[:, d, g, :], Yi[:, d, q, :], ghm[:, gh:gh + 1], -1.0, ALU.mult, ALU.mult),
                               ["Y", "s_ghm"], ["WcI"])
                    n = 0
                    for d in range(2):
                        for comp, XT in ((0, XTr), (1, XTi)):
                            for q in range(8):
                                sl = n % 2
                                n += 1
                                kb.op("pe", lambda e, sl=sl, d=d, q=q, XT=XT: e.transpose(pw[sl][:, 0:128], XT[:, d, q, :], ident_f[:]),
                                      reads=["XT", "ident_f"], writes=[("pw", sl)])
                                kb.op("act", lambda e, sl=sl, d=d, comp=comp, q=q: e.activation(out=Wb[:, d, comp, q, :], in_=pw[sl][:, 0:128], func=AF.Copy),
                                      reads=[("pw", sl)], writes=["Wb"])
                est.close()
                kb.barrier()
                U = S(es, "s_U", [128, 16, NCH], BF16)
                SR = S(es, "s_SR", [128, 8, NCH]); SI = S(es, "s_SI", [128, 8, NCH])
                TR = S(es, "s_TR", [128, 8, NCH]); TI = S(es, "s_TI", [128, 8, NCH])
                Sb = S(es, "s_Sb", [128, 2, 2, 8, NCH + 1], BF16)
                G = S(es, "s_G", [128, 8, 256], BF16)
                GT = S(es, "s_GT", [128, 2, 1024], BF16)
                ya = S(es, "s_ya", [128, 2048]); yb = S(es, "s_yb", [128, 2048])
                sgm = S(es, "s_sgm", [128, 2, 512])
                so = S(es, "s_so", [128, 2, 1024], BF16)
                with ES() as es2:
                    pzr = PS(es2, "s_pzr", [128, 512]); pzi = PS(es2, "s_pzi", [128, 512])
                    py = PS(es2, "s_py", [128, 2048])
                    pgt = PS(es2, "s_pgt", [128, 1024], BF16)
                    pgl = PS(es2, "s_pgl", [128, 512])
                    for b in range(NB):
                        kb.dma("sp", U[:], U_d[b, :, :, :], reads=[], writes=["U"], anchor="ul")
                        for d in range(2):
                            if d == 0:
                                segl = [(0, NCC, 0), (NCC, NLC, NCC)]
                            else:
                                segl = [(NCC, NLC, 0), (0, NCC, NLC)]
                            for q in range(8):
                                for (s0, n_, d0) in segl:
                                    for c0 in range(0, n_, 512):
                                        nn = min(512, n_ - c0)
                                        for comp, pz_, dstS in ((0, pzr, SR), (1, pzi, SI)):
                                            for gh in range(2):
                                                kb.op("pe", lambda e, comp=comp, gh=gh, pz_=pz_, s0=s0, c0=c0, nn=nn, d=d, q=q: e.matmul(
                                                    pz_[gh * 64:(gh + 1) * 64, 0:nn], lhsT=Wb[:, d, comp, q, gh * 64:(gh + 1) * 64],
                                                    rhs=U[:, gh * 8 + q, s0 + c0:s0 + c0 + nn], start=True, stop=True),
                                                    reads=["Wb", "U"], writes=["pz%d" % comp])
                                            kb.op("act", lambda e, pz_=pz_, dstS=dstS, q=q, d0=d0, c0=c0, nn=nn: e.activation(
                                                out=dstS[:, q, d0 + c0:d0 + c0 + nn], in_=pz_[:, 0:nn], func=AF.Copy),
                                                reads=["pz%d" % comp], writes=[("S", q)])
                            cur = (SR, SI); nxt = (TR, TI)
                            ck = "S"; nk = "T"
                            for k in range(nlev):
                                sh = 1 << k
                                if sh >= NCH:
                                    break
                                cr_, ci_ = cur[0], cur[1]
                                nr_, ni_ = nxt[0], nxt[1]
                                if d == 0:
                                    o_sl = slice(sh, NCH); s_sl = slice(0, NCH - sh); keep = slice(0, sh)
                                else:
                                    o_sl = slice(0, NCH - sh); s_sl = slice(sh, NCH); keep = slice(NCH - sh, NCH)
                                for q in range(8):
                                    r0 = kb._r((nk, q))
                                    base = dict(r0[0])
                                    for sname, v in r0[1].items():
                                        if base.get(sname, 0) < v:
                                            base[sname] = v
                                    for j in range(4):
                                        kb.res[(nk, q, j)] = [dict(base), {}]
                                for q in range(8):
                                    col = d * 8 + q
                                    pr_ = Pr[:, k, col:col + 1]
                                    rk = [(ck, q), "P", "Pn"]
                                    dv(lambda e, q=q, pr_=pr_: e.scalar_tensor_tensor(
                                        nr_[:, q, o_sl], cr_[:, q, s_sl], pr_, cr_[:, q, o_sl], ALU.mult, ALU.add), rk + [(nk, q, 0)], [(nk, q, 0)])
                                    dv(lambda e, q=q, pr_=pr_: e.scalar_tensor_tensor(
                                        ni_[:, q, o_sl], ci_[:, q, s_sl], pr_, ci_[:, q, o_sl], ALU.mult, ALU.add), rk + [(nk, q, 1)], [(nk, q, 1)])
                                    kb.op("act", lambda e, q=q: e.activation(out=nr_[:, q, keep], in_=cr_[:, q, keep], func=AF.Copy),
                                          reads=[(ck, q)], writes=[(nk, q, 2)])
                                    kb.op("act", lambda e, q=q: e.activation(out=ni_[:, q, keep], in_=ci_[:, q, keep], func=AF.Copy),
                                          reads=[(ck, q)], writes=[(nk, q, 3)])
                                for q in range(8):
                                    col = d * 8 + q
                                    pi_ = Pi[:, k, col:col + 1]; pn_ = Pn[:, k, col:col + 1]
                                    rk = [(ck, q), "P", "Pn"]
                                    dv(lambda e, q=q, pn_=pn_: e.scalar_tensor_tensor(
                                        nr_[:, q, o_sl], ci_[:, q, s_sl], pn_, nr_[:, q, o_sl], ALU.mult, ALU.add), rk + [(nk, q, 0)], [(nk, q, 0)])
                                    dv(lambda e, q=q, pi_=pi_: e.scalar_tensor_tensor(
                                        ni_[:, q, o_sl], cr_[:, q, s_sl], pi_, ni_[:, q, o_sl], ALU.mult, ALU.add), rk + [(nk, q, 1)], [(nk, q, 1)])
                                for q in range(8):
                                    r0 = kb._r((nk, q))
                                    r0[0] = {}
                                    r0[1] = {}
                                    for j in range(4):
                                        for sname, v in kb._r((nk, q, j))[0].items():
                                            if r0[0].get(sname, 0) < v:
                                                r0[0][sname] = v
                                cur, nxt = nxt, cur
                                ck, nk = nk, ck
                            for comp in range(2):
                                src = cur[comp]
                                if d == 0:
                                    kb.op("pool", lambda e, comp=comp: e.memset(Sb[:, d, comp, :, 0:1], 0.0), reads=[], writes=[("Sb", d, comp)])
                                    kb.op("act", lambda e, comp=comp, src=src: e.activation(out=Sb[:, 0, comp, :, 1:NCH + 1], in_=src[:, :, 0:NCH], func=AF.Copy),
                                          reads=[(ck, q) for q in range(8)], writes=[("Sb", d, comp)])
                                else:
                                    kb.op("pool", lambda e, comp=comp: e.memset(Sb[:, 1, comp, :, NCH - 1:NCH + 1], 0.0), reads=[], writes=[("Sb", d, comp)])
                                    kb.op("act", lambda e, comp=comp, src=src: e.activation(out=Sb[:, 1, comp, :, 0:NCH - 1], in_=src[:, :, 1:NCH], func=AF.Copy),
                                          reads=[(ck, q) for q in range(8)], writes=[("Sb", d, comp)])
                        blocks = []
                        if ctx_out:
                            blocks.append((0, NCC))
                        for c0 in range(0, NLC, 128):
                            blocks.append((NCC + c0, 128))
                        sbk = [("Sb", dd, cc) for dd in range(2) for cc in range(2)]
                        for (m0, nb_) in blocks:
                            is_ctx = m0 < NCC
                            for g in range(16):
                                q = g % 8
                                mb0 = (m0 - NCC) if not is_ctx else (NLC + m0)
                                ops_ = [(U[:, g, m0:m0 + nb_], Toep[:, 0, g, :]),
                                        (Sb[:, 0, 0, q, m0:m0 + nb_], WcR[:, 0, g, :]),
                                        (Sb[:, 0, 1, q, m0:m0 + nb_], WcI[:, 0, g, :]),
                                        (U[:, g, m0:m0 + nb_], Toep[:, 1, g, :]),
                                        (Sb[:, 1, 0, q, mb0:mb0 + nb_], WcR[:, 1, g, :]),
                                        (Sb[:, 1, 1, q, mb0:mb0 + nb_], WcI[:, 1, g, :])]
                                for oi, (lh, rh) in enumerate(ops_):
                                    kb.op("pe", lambda e, lh=lh, rh=rh, oi=oi, g=g: e.matmul(
                                        py[0:nb_, g * 128:(g + 1) * 128], lhsT=lh, rhs=rh, start=(oi == 0), stop=(oi == 5)),
                                        reads=["U", "Toep", "WcR", "WcI"] + sbk, writes=["py"])
                            kb.op("act", lambda e: e.activation(out=ya[0:nb_, :], in_=py[0:nb_, :], func=AF.Square), reads=["py"], writes=["ya"])
                            dv(lambda e: e.tensor_scalar(ya[0:nb_, :], ya[0:nb_, :], 0.044715, 1.0, ALU.mult, ALU.add), ["ya"], ["ya"])
                            dv(lambda e: e.tensor_tensor(ya[0:nb_, :], ya[0:nb_, :], py[0:nb_, :], ALU.mult), ["ya", "py"], ["ya"])
                            kb.op("act", lambda e: e.activation(out=yb[0:nb_, :], in_=ya[0:nb_, :], func=AF.Sigmoid, scale=1.5957691216057308),
                                  reads=["ya"], writes=["yb"])
                            dv(lambda e: e.tensor_tensor(G[0:nb_, :, :].rearrange("p j (g h) -> p g j h", g=16),
                                                         yb[0:nb_, :].rearrange("p (g j h) -> p g j h", g=16, j=8),
                                                         py[0:nb_, :].rearrange("p (g j h) -> p g j h", g=16, j=8), ALU.mult),
                               ["yb", "py"], ["G"])
                            ntok = nb_ * 8
                            for hf in range(2):
                                for j in range(8):
                                    kb.op("pe", lambda e, hf=hf, j=j: e.transpose(pgt[:, j * 128:j * 128 + nb_], G[0:nb_, j, hf * 128:(hf + 1) * 128],
                                                                                  ident_b[0:nb_, 0:nb_]), reads=["G", "ident_b"], writes=["pgt"])
                                kb.op("act", lambda e, hf=hf: e.activation(
                                    out=GT[:, hf, 0:ntok].rearrange("p (c j) -> p j c", j=8),
                                    in_=pgt[:, :].rearrange("p (j c) -> p j c", j=8)[:, :, 0:nb_], func=AF.Copy),
                                    reads=["pgt"], writes=["GT"])
                            for t0 in range(0, ntok, 512):
                                nn = min(512, ntok - t0)
                                for ho in range(2):
                                    for hi in range(2):
                                        kb.op("pe", lambda e, ho=ho, hi=hi, t0=t0, nn=nn: e.matmul(
                                            pgl[:, 0:nn], lhsT=wgl[:, hi, ho * 128:(ho + 1) * 128], rhs=GT[:, hi, t0:t0 + nn],
                                            start=(hi == 0), stop=(hi == 1)), reads=["s_wgl", "GT"], writes=["pgl"])
                                    kb.op("act", lambda e, ho=ho, nn=nn: e.activation(out=sgm[:, ho, 0:nn], in_=pgl[:, 0:nn], func=AF.Sigmoid,
                                                                                      bias=bglu[:, ho:ho + 1]), reads=["pgl", "s_bglu"], writes=["sgm"])
                                    dv(lambda e, ho=ho, t0=t0, nn=nn: e.tensor_tensor(so[:, ho, t0:t0 + nn], sgm[:, ho, 0:nn], GT[:, ho, t0:t0 + nn], ALU.mult),
                                       ["sgm", "GT"], ["so"])
                            p0 = m0 * 8
                            kb.dma("sp", mix_d[b, 4:6, :, p0:p0 + ntok].rearrange("c p n -> p c n"), so[:, :, 0:ntok],
                                   reads=["so"], writes=[("mix_d", b, "ssm", p0)], anchor="sst")
            kb.barrier()

        def attn_phase(l, ctx_out):
            with ES() as es:
                qT = S(es, "a_qT", [128, 4, NP], BF16)
                kTe = S(es, "a_kTe", [128, 2, NP], BF16)
                kTo = S(es, "a_kTo", [128, 2, NP], BF16)
                v2e = S(es, "a_v2e", [128, NKC, 2, 128], BF16)
                v2o = S(es, "a_v2o", [128, NKC, 2, 128], BF16)
                onesE = S(es, "a_onesE", [128, 128], BF16)
                onesO = S(es, "a_onesO", [128, 128], BF16)
                mprev = S(es, "a_mprev", [128, 128], BF16); mnext = S(es, "a_mnext", [128, 128], BF16)
                mf32 = S(es, "a_mf32", [128, 256])
                esk = S(es, "a_esk", [128, 8])
                esb = S(es, "a_esb", [128, 2, 256])
                pT = [S(es, "a_pT%d" % i, [128, 512], BF16) for i in range(4)]
                dsb = S(es, "a_dsb", [128, 256])
                ao = [S(es, "a_ao%d" % i, [128, 2, 128], BF16) for i in range(2)]
                cw = S(es, "a_cw", [128, 6])
                gz = S(es, "a_gz", [128, 2, NP + 4], BF16)
                gb = S(es, "a_gb", [128, 2, NP], BF16)
                ctmp = S(es, "a_ctmp", [128, L])
                cvo = S(es, "a_cvo", [128, 2, NP], BF16)
                kb.op("dve", lambda e: e.memset(onesE[:], 0.0), writes=["a_ones"])
                kb.op("dve", lambda e: e.memset(onesO[:], 0.0), writes=["a_ones"])
                kb.op("dve", lambda e: e.memset(onesE[:, 0:64], 1.0), reads=["a_ones"], writes=["a_ones"])
                kb.op("dve", lambda e: e.memset(onesO[:, 64:128], 1.0), reads=["a_ones"], writes=["a_ones"])
                kb.op("dve", lambda e: e.memset(v2e[:, :, :, 64:128], 0.0), writes=["a_v2z"])
                kb.op("dve", lambda e: e.memset(v2o[:, :, :, 0:64], 0.0), writes=["a_v2z"])
                kb.op("dve", lambda e: e.memset(kTe[64:128, :, :], 0.0), writes=["kTe_z"])
                kb.op("dve", lambda e: e.memset(kTo[0:64, :, :], 0.0), writes=["kTo_z"])
                kb.dma("sp", mf32[:, 0:128], mprev_in[:, :], writes=["a_mf32"], anchor="a0")
                kb.dma("sp", mf32[:, 128:256], mnext_in[:, :], writes=["a_mf32"], anchor="a0")
                kb.op("dve", lambda e: e.tensor_copy(mprev[:], mf32[:, 0:128]), reads=["a_mf32"], writes=["a_mprev"])
                kb.op("dve", lambda e: e.tensor_copy(mnext[:], mf32[:, 128:256]), reads=["a_mf32"], writes=["a_mnext"])
                kb.dma("sp", esk[:], sink_in[l, :, :], writes=["a_esk"], anchor="a1")
                kb.op("act", lambda e: e.activation(out=esk[:], in_=esk[:], func=AF.Exp), reads=["a_esk"], writes=["a_esk"])
                for kh in range(2):
                    for pp in range(2):
                        for par in range(2):
                            hd = 4 * kh + 2 * pp + par
                            kb.op("dve", lambda e, kh=kh, pp=pp, par=par, hd=hd: e.tensor_copy(
                                esb[par * 64:(par + 1) * 64, kh, pp * 128:(pp + 1) * 128],
                                esk[par * 64:(par + 1) * 64, hd:hd + 1].to_broadcast([64, 128])),
                                reads=["a_esk"], writes=["a_esb"])
                kb.dma("sp", cw[:], convw_in[l, :, :], writes=["a_cw"], anchor="a2")
                with ES() as es2:
                    pst = [PS(es2, "a_pst%d" % i, [128, 512]) for i in range(3)]
                    pnum = [PS(es2, "a_pnum%d" % i, [128, 512]) for i in range(2)]
                    pden = [PS(es2, "a_pden%d" % i, [128, 512]) for i in range(2)]
                    pst = pst + [PS(es2, "a_pst3", [128, 512])]
                    for b in range(NB):
                        kb.dma("sp", qT[:], qT_d[b, :, :, :].rearrange("q p n -> p q n"), writes=["a_qT"], anchor="a3")
                        for kh in range(2):
                            kb.dma("sp", kTe[0:64, kh, :], kT_d[b, kh * 64:(kh + 1) * 64, :], reads=["kTe_z"], writes=["a_kT"], anchor="a4")
                            kb.dma("sp", kTo[64:128, kh, :], kT_d[b, kh * 64:(kh + 1) * 64, :], reads=["kTo_z"], writes=["a_kT"], anchor="a4")
                            for dup, vt in ((0, v2e), (1, v2o)):
                                kb.dma("sp", vt[:, :, kh, dup * 64:(dup + 1) * 64],
                                       v_d[b, :, kh * 64:(kh + 1) * 64].rearrange("(c p) d -> p c d", p=128),
                                       reads=["a_v2z"], writes=["a_v2"], anchor="a5")
                        qblocks = []
                        if ctx_out:
                            for n in range(NCTX // 128):
                                qblocks.append((n * 128, [(0, None), (1, None)]))
                        ncl = NCTX // 128
                        for n in range(NT):
                            kl = [(0, None), (1, None)]
                            if n >= 1:
                                kl.append((ncl + n - 1, mprev))
                            kl.append((ncl + n, None))
                            if n + 1 < NT:
                                kl.append((ncl + n + 1, mnext))
                            qblocks.append((NCTX + n * 128, kl))
                        items = []
                        it = 0
                        for (qp, kl) in qblocks:
                            for kh in range(2):
                                ns = it % 2
                                it += 1
                                for ki, (kc, msk) in enumerate(kl):
                                    items.append((qp, kh, ns, ki, kc, msk, len(kl)))

                        def emit_qk(idx):
                            qp, kh, ns, ki, kc, msk, nk = items[idx]
                            sl = idx % 4
                            for par, kt in ((0, kTe), (1, kTo)):
                                kb.op("pe", lambda e, par=par, kt=kt: e.matmul(
                                    pst[sl][:, par * 256:(par + 1) * 256], lhsT=kt[:, kh, kc * 128:(kc + 1) * 128],
                                    rhs=qT[:, 2 * kh:2 * kh + 2, qp:qp + 128], start=True, stop=True),
                                    reads=["a_kT", "a_qT"], writes=[("pst", sl)])
                            kb.op("act", lambda e: e.activation(out=pT[sl][:], in_=pst[sl][:], func=AF.Exp, scale=0.125),
                                  reads=[("pst", sl)], writes=[("pT", sl)])
                            if msk is not None:
                                kb.op("dve", lambda e: e.tensor_tensor(
                                    pT[sl][:, :].rearrange("p (h q) -> p h q", h=4), pT[sl][:, :].rearrange("p (h q) -> p h q", h=4),
                                    msk[:, :].unsqueeze(1).to_broadcast([128, 4, 128]), ALU.mult),
                                    reads=[("pT", sl), "a_mprev", "a_mnext"], writes=[("pT", sl)])

                        def emit_pv(idx):
                            qp, kh, ns, ki, kc, msk, nk = items[idx]
                            sl = idx % 4
                            for par, vt in ((0, v2e), (1, v2o)):
                                kb.op("pe", lambda e, par=par, vt=vt: e.matmul(
                                    pnum[ns][:, 0:256], lhsT=vt[:, kc, kh, :], rhs=pT[sl][:, par * 256:(par + 1) * 256],
                                    start=(ki == 0 and par == 0), stop=(ki == nk - 1 and par == 1)),
                                    reads=["a_v2", ("pT", sl)], writes=[("pnum", ns)])
                            for par, on in ((0, onesE), (1, onesO)):
                                kb.op("pe", lambda e, par=par, on=on: e.matmul(
                                    pden[ns][:, 0:256], lhsT=on[:], rhs=pT[sl][:, par * 256:(par + 1) * 256],
                                    start=(ki == 0 and par == 0), stop=(ki == nk - 1 and par == 1)),
                                    reads=["a_ones", ("pT", sl)], writes=[("pden", ns)])
                            if ki == nk - 1:
                                kb.op("dve", lambda e: e.tensor_tensor(dsb[:], pden[ns][:, 0:256], esb[:, kh, :], ALU.add),
                                      reads=[("pden", ns), "a_esb"], writes=["a_dsb"])
                                kb.op("dve", lambda e: e.reciprocal(dsb[:], dsb[:]), reads=["a_dsb"], writes=["a_dsb"])
                                kb.op("dve", lambda e: e.tensor_tensor(ao[ns][:, :, :].rearrange("p h q -> p (h q)"), pnum[ns][:, 0:256],
                                                                       dsb[:], ALU.mult),
                                      reads=[("pnum", ns), "a_dsb"], writes=[("ao", ns)])
                                kb.dma("sp", mix_d[b, 2 * kh:2 * kh + 2, :, qp:qp + 128].rearrange("c p n -> p c n"), ao[ns][:],
                                       reads=[("ao", ns)], writes=[("mix_d", b, "att", kh, qp)], anchor="ao%d" % ns)

                        LAG = 3
                        for idx in range(len(items)):
                            emit_qk(idx)
                            if idx >= LAG:
                                emit_pv(idx - LAG)
                        for idx in range(max(0, len(items) - LAG), len(items)):
                            emit_pv(idx)
                        kb.op("dve", lambda e: e.memset(gz[:, :, 0:1], 0.0), writes=["a_gzp"])
                        kb.op("dve", lambda e: e.memset(gz[:, :, NCTX + 1:NCTX + 3], 0.0), writes=["a_gzp"])
                        kb.op("dve", lambda e: e.memset(gz[:, :, NP + 3:NP + 4], 0.0), writes=["a_gzp"])
                        kb.dma("sp", gz[:, :, 1:NCTX + 1], gz_d[b, :, :, 0:NCTX].rearrange("c p n -> p c n"), reads=["a_gzp"], writes=["a_gz"], anchor="a6")
                        kb.dma("sp", gz[:, :, NCTX + 3:NP + 3], gz_d[b, :, :, NCTX:NP].rearrange("c p n -> p c n"), reads=["a_gzp"], writes=["a_gz"], anchor="a6")
                        kb.dma("sp", gb[:], gb_d[b, :, :, :].rearrange("c p n -> p c n"), writes=["a_gb"], anchor="a7")
                        sgl = [(NCTX + 3, NCTX, L)]
                        if ctx_out:
                            sgl.append((1, 0, NCTX))
                        for (g0, o0, n_) in sgl:
                            for hf in range(2):
                                kb.op("dve", lambda e, hf=hf, g0=g0, n_=n_: e.tensor_scalar(ctmp[:, 0:n_], gz[:, hf, g0 - 1:g0 - 1 + n_], cw[:, hf * 3:hf * 3 + 1], None, ALU.mult),
                                      reads=["a_gz", "a_gzp", "a_cw"], writes=["a_ctmp"])
                                for kk in (1, 2):
                                    kb.op("dve", lambda e, hf=hf, g0=g0, n_=n_, kk=kk: e.scalar_tensor_tensor(
                                        ctmp[:, 0:n_], gz[:, hf, g0 - 1 + kk:g0 - 1 + kk + n_], cw[:, hf * 3 + kk:hf * 3 + kk + 1], ctmp[:, 0:n_], ALU.mult, ALU.add),
                                        reads=["a_gz", "a_gzp", "a_cw", "a_ctmp"], writes=["a_ctmp"])
                                kb.op("dve", lambda e, hf=hf, o0=o0, n_=n_: e.tensor_tensor(cvo[:, hf, o0:o0 + n_], ctmp[:, 0:n_], gb[:, hf, o0:o0 + n_], ALU.mult),
                                      reads=["a_ctmp", "a_gb"], writes=["a_cvo"])
                            kb.dma("sp", mix_d[b, 6:8, :, o0:o0 + n_].rearrange("c p n -> p c n"), cvo[:, :, o0:o0 + n_],
                                   reads=["a_cvo"], writes=[("mix_d", b, "conv", o0)], anchor="a8")
            kb.barrier()

        def mix_out(l, ctx_out):
            with ES() as es:
                wo = S(es, "wo", [128, KC, D], BF16)
                kb.dma("pool", wo[:], w_out[l, :, :].rearrange("(c p) n -> p c n", p=128), writes=["wo"], anchor="wo")
                A, gv = load_rowconsts(es, l, 1, 1.0)
                mx = [S(es, "mx%d" % i, [128, 8, 512], BF16) for i in range(2)]
                xin = [S(es, "xin%d" % i, [128, 4, D]) for i in range(2)]
                junk = S(es, "junk", [128, D], BF16)
                ss2 = S(es, "ss2", [128, 8]); rstd2 = S(es, "rstd2", [128, 8])
                tt = S(es, "tt", [128, D])
                pd = [PS(es, "pd%d" % i, [128, 1024]) for i in range(2)]
                groups = []
                for (b, r, p0, n) in segs(ctx_out, True):
                    for g0 in range(0, n, 512):
                        groups.append((b, r, p0 + g0, min(512, n - g0)))

                def load(gi):
                    b, r, pos, n = groups[gi]
                    sl = gi % 2
                    kb.dma("sp", xin[sl][:, 0:n // 128, :], xres[b, pos:pos + n, :].rearrange("(t p) d -> p t d", p=128),
                           writes=[("xin", sl)], anchor="xl%d" % sl)
                    kb.dma("sp", mx[sl][:, :, 0:n], mix_d[b, :, :, pos:pos + n].rearrange("c p n -> p c n"),
                           writes=[("mx", sl)], anchor="ml%d" % sl)
                load(0)
                for gi, (b, r, pos, n) in enumerate(groups):
                    sl = gi % 2
                    if gi + 1 < len(groups):
                        load(gi + 1)
                    for t in range(n // 128):
                        ps = t % 2
                        for nh in range(2):
                            for c in range(KC):
                                kb.op("pe", lambda e, c=c, nh=nh, t=t, ps=ps, sl=sl: e.matmul(
                                    pd[ps][:, nh * 512:(nh + 1) * 512], lhsT=mx[sl][:, c, t * 128:(t + 1) * 128],
                                    rhs=wo[:, c, nh * 512:(nh + 1) * 512], start=(c == 0), stop=(c == KC - 1)),
                                    reads=[("mx", sl), "wo"], writes=[("pd", ps)])
                        post_residual(pd[ps], ("pd", ps), xin[sl][:, t, :], ("xin", sl), gv[r], ("gvec", r), junk, ss2, rstd2, tt, t)
                    kb.dma("pool", xres[b, pos:pos + n, :].rearrange("(t p) d -> p t d", p=128), xin[sl][:, 0:n // 128, :],
                           reads=[("xin", sl)], writes=[("xst", b, pos)], anchor="pxs%d" % sl)
            kb.barrier()

        import os
        stop = int(os.environ.get("KSTOP", "99"))
        ph = 0
        for l in range(depth):
            last = (l == depth - 1)
            for fn in (lambda: ffn_phase(l, 0, 0, True, False), lambda: mix_proj(l, not last), lambda: ssm_phase(l, not last),
                       lambda: attn_phase(l, not last), lambda: mix_out(l, not last), lambda: ffn_phase(l, 2, 1, not last, last)):
                ph += 1
                if ph <= stop:
                    fn()
        kb.barrier()
        if stop < 6 * depth:
            for b in range(NB):
                kb.dma("sp", y_out[b, :, :], xres[b, NCTX:NP, :], writes=[("ydump", b)], anchor="cp")
            kb.barrier()
    return nc


def _consts(L):
    NT = L // 128
    c = {}
    c["ident"] = np.eye(128, dtype=np.float32)
    j = np.arange(128)[:, None]
    i = np.arange(128)[None, :]
    c["m_prev"] = (j >= i).astype(np.float32)
    c["m_next"] = (j <= i).astype(np.float32)
    bi = (np.arange(128) // 16)
    c["tmask_f"] = (bi[None, :] >= bi[:, None]).astype(np.float32)
    c["tmask_b"] = (bi[:, None] >= bi[None, :]).astype(np.float32)
    gh = np.zeros((128, 2), np.float32)
    gh[:64, 0] = 1.0
    gh[64:, 1] = 1.0
    c["gh_mask"] = gh
    pos = np.arange(L, dtype=np.float32)
    row = np.floor(pos / 64.0).astype(np.float32)
    col = (pos - row * 64.0).astype(np.float32)
    inv = (np.float32(10000.0) ** (-np.arange(16, dtype=np.float32) / np.float32(16.0))).astype(np.float32)
    ang = np.stack([row[:, None] * inv[None, :], col[:, None] * inv[None, :]], axis=1).astype(np.float32)
    tab = np.concatenate([np.cos(ang).reshape(L, 32), np.sin(ang).reshape(L, 32)], axis=1).astype(np.float32)
    c["rope"] = np.ascontiguousarray(tab.reshape(NT, 128, 64).transpose(1, 0, 2))
    return c


def _shared_inputs(inp, depth, L):
    f = lambda a: np.ascontiguousarray(np.asarray(a, dtype=np.float32))
    sh = {}
    for k in ("w_ada", "b_ada", "norm_post", "ffn_w_gate", "ffn_w_up", "ffn_w_down", "w_in", "w_out"):
        sh[k] = f(inp[k])[:depth]
    npre = f(inp["norm_pre"])[:depth]
    sh["gpreT"] = np.ascontiguousarray(npre.reshape(depth * 3, KC, 128).transpose(2, 0, 1))
    sh["sink"] = np.ascontiguousarray(np.broadcast_to(f(inp["attn_sink"])[:depth, None, :], (depth, 128, 8)))

    def dq(a):
        a = f(a)[:depth].reshape(depth, 2, 2, 8, 64)
        return np.ascontiguousarray(a.transpose(0, 2, 4, 1, 3).reshape(depth, 128, 16))
    sh["lam_re"] = dq(inp["ssm_lambda_re"])
    sh["lam_im"] = dq(inp["ssm_lambda_im"])
    ls = f(inp["ssm_log_step"])[:depth]
    sh["lstep"] = dq(np.broadcast_to(ls[..., None], ls.shape + (64,)))

    def bq(a):
        a = f(a)[:depth].reshape(depth, 2, 2, 8, 64, 16)
        return np.ascontiguousarray(a.transpose(0, 2, 4, 1, 3, 5).reshape(depth, 128, 256))

    def cq(a):
        a = f(a)[:depth].reshape(depth, 2, 2, 8, 16, 64)
        return np.ascontiguousarray(a.transpose(0, 2, 5, 1, 3, 4).reshape(depth, 128, 256))
    sh["b_re"] = bq(inp["ssm_b_re"]); sh["b_im"] = bq(inp["ssm_b_im"])
    sh["c_re"] = cq(inp["ssm_c_re"]); sh["c_im"] = cq(inp["ssm_c_im"])
    dsk = f(inp["ssm_d"])[:depth].reshape(depth, 16, 16)
    sh["d_skip"] = np.ascontiguousarray(np.broadcast_to(dsk.transpose(0, 2, 1)[:, None, :, :], (depth, 8, 16, 16)).reshape(depth, 128, 16))
    sh["w_glu"] = f(inp["ssm_w_glu"])[:depth]
    sh["b_glu"] = np.ascontiguousarray(f(inp["ssm_b_glu"])[:depth].reshape(depth, 2, 128).transpose(0, 2, 1))
    cw = f(inp["conv_w"])[:depth]
    sh["conv_w"] = np.ascontiguousarray(cw.reshape(depth, 3, 2, 128).transpose(0, 3, 2, 1).reshape(depth, 128, 6))
    sh.update(_consts(L))
    return sh


_CACHE = {}


def run(inp, depth=DEPTH, n_cores=8, trace=False):
    x = np.asarray(inp["x"], dtype=np.float32)
    L = x.shape[1]
    key = (L, depth)
    if key not in _CACHE:
        _CACHE[key] = build(L, depth)
    nc = _CACHE[key]
    sh = _shared_inputs(inp, depth, L)
    c = np.asarray(inp["c"], dtype=np.float32)
    ctx = np.asarray(inp["ctx"], dtype=np.float32)
    cc = np.asarray(inp["c_ctx"], dtype=np.float32)
    in_maps = []
    for i in range(n_cores):
        m = dict(sh)
        m["x"] = np.ascontiguousarray(x[NB * i:NB * i + NB])
        m["ctx"] = np.ascontiguousarray(ctx[NB * i:NB * i + NB])
        cv = np.stack([c[NB * i], c[NB * i + 1], cc], axis=0)
        m["cT"] = np.ascontiguousarray(cv.reshape(3, KC, 128).transpose(2, 1, 0))
        in_maps.append(m)
    res = run_bass_kernel_spmd(nc, in_maps, core_ids=list(range(n_cores)))
    return np.concatenate([np.asarray(r["y"], dtype=np.float32) for r in res.results], axis=0)


def kernel(**inputs):
    return run(inputs, DEPTH, 8)
```
